# Optimizing a Trainium2 kernel written in Bass

```python
import math
import jax, jax.numpy as jnp
from jax import lax
import numpy as np


D_MODEL = 1024
BATCH = 4
SEQ = 8192
DEPTH = 1

N_HEADS_A = 8
HEAD_DIM = 64
WIDTH_A = N_HEADS_A * HEAD_DIM
MOBA_BLOCK = 256
MOBA_TOPK = 3
Q_CHUNK = 64
N_POOL_GROUPS = 4
POOL_WINDOWS = (2, 4, 8, 16)
WIDTH_B = D_MODEL // 2
POOL_GROUP_DIM = WIDTH_B // N_POOL_GROUPS
N_BRANCHES = 2
D_FF = 4 * D_MODEL
REL_BUCKETS = 32
REL_MAX_DIST = 128
DEEPNORM_ALPHA = (2.0 * DEPTH) ** 0.25
DEEPNORM_BETA = (8.0 * DEPTH) ** -0.25
LN_EPS = 1e-5
NEG_INF = -1e30
IN_COLS = 3 * WIDTH_A + WIDTH_B + N_BRANCHES * D_MODEL

kernel_name = 'moba_pool_hybrid_deepnorm'


def layer_norm(x, g, b):
    xf = x.astype(jnp.float32)
    mu = jnp.mean(xf, axis=-1, keepdims=True)
    var = jnp.mean(jnp.square(xf - mu), axis=-1, keepdims=True)
    return ((xf - mu) * lax.rsqrt(var + LN_EPS) * g + b).astype(x.dtype)


def rel_bucket(dist):
    n = jnp.maximum(dist, 0)
    max_exact = REL_BUCKETS // 2
    nf = jnp.maximum(n, 1).astype(jnp.float32)
    large = max_exact + (jnp.log(nf / max_exact) / math.log(REL_MAX_DIST / max_exact)
                         * (REL_BUCKETS - max_exact)).astype(jnp.int32)
    large = jnp.minimum(large, REL_BUCKETS - 1)
    return jnp.where(n < max_exact, n, large)


def moba_attention(q, k, v, rel_table):
    B, H, S, Dh = q.shape
    nb = -(-S // MOBA_BLOCK)
    s_pad = nb * MOBA_BLOCK
    pad = ((0, 0), (0, 0), (0, s_pad - S), (0, 0))
    q = jnp.pad(q, pad)
    k = jnp.pad(k, pad)
    v = jnp.pad(v, pad)
    k_blk = k.reshape(B, H, nb, MOBA_BLOCK, Dh)
    v_blk = v.reshape(B, H, nb, MOBA_BLOCK, Dh)
    k_mean = jnp.mean(k_blk.astype(jnp.float32), axis=3).astype(k.dtype)
    topk = min(MOBA_TOPK, nb)
    n_chunks = s_pad // Q_CHUNK
    q_ch = q.reshape(B, H, n_chunks, Q_CHUNK, Dh).transpose(2, 0, 1, 3, 4)
    table_t = rel_table.T
    scale = HEAD_DIM ** -0.5
    b_ix = jnp.arange(B)[:, None, None, None]
    h_ix = jnp.arange(H)[None, :, None, None]
    offs = jnp.arange(MOBA_BLOCK)
    blk_ids = jnp.arange(nb)

    def one_chunk(args):
        qc, c = args
        q_pos = c * Q_CHUNK + jnp.arange(Q_CHUNK)
        blk = (c * Q_CHUNK) // MOBA_BLOCK
        gate = jnp.einsum('bhqd,bhnd->bhqn', qc, k_mean,
                          preferred_element_type=jnp.float32)
        gate = jnp.where(blk_ids < blk, gate, NEG_INF)
        _, idx = lax.top_k(gate, topk)
        sel_valid = idx < blk
        k_sel = k_blk[b_ix, h_ix, idx]
        v_sel = v_blk[b_ix, h_ix, idx]
        s_sel = jnp.einsum('bhqd,bhqnld->bhqnl', qc, k_sel,
                           preferred_element_type=jnp.float32) * scale
        k_pos_sel = idx[..., None] * MOBA_BLOCK + offs
        bucket_sel = rel_bucket(q_pos[None, None, :, None, None] - k_pos_sel)
        s_sel = s_sel + table_t[h_ix[..., None], bucket_sel]
        s_sel = jnp.where(sel_valid[..., None], s_sel, NEG_INF)
        k_own = lax.dynamic_index_in_dim(k_blk, blk, axis=2, keepdims=False)
        v_own = lax.dynamic_index_in_dim(v_blk, blk, axis=2, keepdims=False)
        dist_own = q_pos[:, None] - (blk * MOBA_BLOCK + offs)[None, :]
        s_own = jnp.einsum('bhqd,bhld->bhql', qc, k_own,
                           preferred_element_type=jnp.float32) * scale
        s_own = s_own + table_t[:, rel_bucket(dist_own)]
        s_own = jnp.where(dist_own >= 0, s_own, NEG_INF)
        logits = jnp.concatenate(
            [s_sel.reshape(B, H, Q_CHUNK, topk * MOBA_BLOCK), s_own], axis=-1)
        p = jax.nn.softmax(logits, axis=-1)
        p_sel = p[..., :topk * MOBA_BLOCK].reshape(B, H, Q_CHUNK, topk, MOBA_BLOCK).astype(v.dtype)
        p_own = p[..., topk * MOBA_BLOCK:].astype(v.dtype)
        return (jnp.einsum('bhqnl,bhqnld->bhqd', p_sel, v_sel)
                + jnp.einsum('bhql,bhld->bhqd', p_own, v_own))

    out = lax.map(one_chunk, (q_ch, jnp.arange(n_chunks)))
    out = out.transpose(1, 2, 0, 3, 4).reshape(B, H, s_pad, Dh)
    return out[:, :, :S]


def pool_mixer(p, w_pool, pool_scale):
    B, S, _ = p.shape
    pf = p.reshape(B, S, N_POOL_GROUPS, POOL_GROUP_DIM).astype(jnp.float32)
    cs = jnp.pad(jnp.cumsum(pf, axis=1), ((0, 0), (1, 0), (0, 0), (0, 0)))
    t = jnp.arange(S)[:, None]
    win = jnp.array(POOL_WINDOWS, dtype=jnp.int32)[None, :]
    lo = jnp.maximum(t + 1 - win, 0)
    g_ix = jnp.arange(N_POOL_GROUPS)[None, :]
    window_sum = cs[:, 1:] - cs[:, lo, g_ix]
    count = jnp.minimum(t + 1, win).astype(jnp.float32)
    pooled = (window_sum / count[None, :, :, None] - pf).astype(p.dtype)
    mixed = jnp.einsum('bsgc,gcd->bsgd', pooled, w_pool)
    return mixed.reshape(B, S, WIDTH_B) * pool_scale


def setup_inputs(seed: int = 0) -> dict:
    key = jax.random.key(seed)
    ks = jax.random.split(key, 20)
    f32 = jnp.float32

    def nrm(k, shape, scale):
        return jax.random.normal(k, shape, f32) * scale

    x = nrm(ks[0], (BATCH, SEQ, D_MODEL), 1.0)
    col_scale = jnp.ones((IN_COLS,), f32).at[2 * WIDTH_A:3 * WIDTH_A].set(DEEPNORM_BETA)
    w_in = nrm(ks[1], (DEPTH, D_MODEL, IN_COLS), D_MODEL ** -0.5) * col_scale
    b_gate = nrm(ks[2], (DEPTH, N_BRANCHES * D_MODEL), 0.1)
    rel_table = nrm(ks[3], (REL_BUCKETS, N_HEADS_A), 0.5)
    w_pool = nrm(ks[4], (DEPTH, N_POOL_GROUPS, POOL_GROUP_DIM, POOL_GROUP_DIM), POOL_GROUP_DIM ** -0.5)
    pool_scale = 1.0 + nrm(ks[5], (DEPTH, WIDTH_B), 0.1)
    w_o_attn = nrm(ks[6], (DEPTH, WIDTH_A, D_MODEL), WIDTH_A ** -0.5 * DEEPNORM_BETA)
    w_o_pool = nrm(ks[7], (DEPTH, WIDTH_B, D_MODEL), WIDTH_B ** -0.5 * DEEPNORM_BETA)
    w_out = nrm(ks[8], (DEPTH, D_MODEL, D_MODEL), D_MODEL ** -0.5 * DEEPNORM_BETA)
    ln1_g = 1.0 + nrm(ks[9], (DEPTH, D_MODEL), 0.05)
    ln1_b = nrm(ks[10], (DEPTH, D_MODEL), 0.05)
    w_ff1 = nrm(ks[11], (DEPTH, D_MODEL, D_FF), D_MODEL ** -0.5 * DEEPNORM_BETA)
    b_ff1 = nrm(ks[12], (DEPTH, D_FF), 0.02)
    w_ff2 = nrm(ks[13], (DEPTH, D_FF, D_MODEL), D_FF ** -0.5 * DEEPNORM_BETA)
    b_ff2 = nrm(ks[14], (DEPTH, D_MODEL), 0.02)
    ln2_g = 1.0 + nrm(ks[15], (DEPTH, D_MODEL), 0.05)
    ln2_b = nrm(ks[16], (DEPTH, D_MODEL), 0.05)
    return {'x': x, 'w_in': w_in, 'b_gate': b_gate, 'rel_table': rel_table,
            'w_pool': w_pool, 'pool_scale': pool_scale, 'w_o_attn': w_o_attn,
            'w_o_pool': w_o_pool, 'w_out': w_out, 'ln1_g': ln1_g, 'ln1_b': ln1_b,
            'w_ff1': w_ff1, 'b_ff1': b_ff1, 'w_ff2': w_ff2, 'b_ff2': b_ff2,
            'ln2_g': ln2_g, 'ln2_b': ln2_b}


def reference(x, w_in, b_gate, rel_table, w_pool, pool_scale, w_o_attn, w_o_pool,
              w_out, ln1_g, ln1_b, w_ff1, b_ff1, w_ff2, b_ff2, ln2_g, ln2_b):
    B, S, _ = x.shape
    h = x
    split_at = [WIDTH_A, 2 * WIDTH_A, 3 * WIDTH_A, 3 * WIDTH_A + WIDTH_B]
    for l in range(DEPTH):
        proj = jnp.einsum('bsd,de->bse', h, w_in[l])
        q, k, v, p, g = jnp.split(proj, split_at, axis=-1)

        def heads(t):
            return t.reshape(B, S, N_HEADS_A, HEAD_DIM).transpose(0, 2, 1, 3)

        a = moba_attention(heads(q), heads(k), heads(v), rel_table)
        a = a.transpose(0, 2, 1, 3).reshape(B, S, WIDTH_A)
        branch_a = a @ w_o_attn[l]
        branch_b = pool_mixer(p, w_pool[l], pool_scale[l]) @ w_o_pool[l]
        gates = jax.nn.sigmoid((g + b_gate[l]).reshape(B, S, N_BRANCHES, D_MODEL))
        mix = (gates[:, :, 0] * branch_a + gates[:, :, 1] * branch_b) @ w_out[l]
        h = layer_norm(DEEPNORM_ALPHA * h + mix, ln1_g[l], ln1_b[l])
        ff = jnp.square(jax.nn.relu(h @ w_ff1[l] + b_ff1[l])) @ w_ff2[l] + b_ff2[l]
        h = layer_norm(DEEPNORM_ALPHA * h + ff, ln2_g[l], ln2_b[l])
    return h
```

```python
from contextlib import ExitStack
import math

import numpy as np
import concourse.bass as bass
import concourse.mybir as mybir
from concourse.bass_utils import run_bass_kernel_spmd

F32 = mybir.dt.float32
BF16 = mybir.dt.bfloat16
AF = mybir.ActivationFunctionType
ALU = mybir.AluOpType

D = 1024
NBLK = 16
BL = 256
TOK = NBLK * BL
DFF = 4096
ALPHA = 2.0 ** 0.25
EPS = 1e-5
NEG = -1.0e30
WINS = (2, 4, 8, 16)
FG = 2
NFG = 32 // FG


class _Op:
    __slots__ = ("eng", "fn", "deps", "is_dma", "semkey", "signal", "sem", "val", "idx")


class Prog:
    ENGS = ("pe", "act", "dve", "pool", "sp")

    def __init__(self):
        self.ops = []
        self.last_w = {}
        self.readers = {}

    def add(self, eng, fn, reads=(), writes=(), dma=None):
        op = _Op()
        op.eng = eng
        op.fn = fn
        op.is_dma = dma is not None
        op.semkey = dma
        op.deps = {}
        op.signal = False
        op.idx = len(self.ops)
        op.sem = None
        op.val = 0
        writes = list(writes) + [r for r in reads if r.startswith("ps") and r not in writes]
        reads = [r for r in reads if not r.startswith("ps")]
        for r in reads:
            w = self.last_w.get(r)
            if w is not None:
                op.deps[w.idx] = (w, "raw")
        for r in writes:
            w = self.last_w.get(r)
            if w is not None and w.idx not in op.deps:
                op.deps[w.idx] = (w, "waw")
            for o in self.readers.get(r, {}).values():
                if o.idx not in op.deps:
                    op.deps[o.idx] = (o, "war")
        for r in reads:
            d = self.readers.setdefault(r, {})
            d[("dma", op.idx) if op.is_dma else eng] = op
        for r in writes:
            self.last_w[r] = op
            self.readers[r] = {}
        self.ops.append(op)
        return op

    @staticmethod
    def needs_wait(o, op, kind):
        if o.is_dma or op.is_dma:
            return True
        if o.eng != op.eng:
            return True
        return op.eng != "pe"

    def emit(self, nc):
        for op in self.ops:
            for (o, kind) in op.deps.values():
                if self.needs_wait(o, op, kind):
                    o.signal = True
        counters = {e: 0 for e in self.ENGS}
        dma_counts = {}
        keys = []
        for op in self.ops:
            if op.is_dma:
                k = ("dma", op.semkey)
                dma_counts[k] = dma_counts.get(k, 0) + 16
                op.sem = k
                op.val = dma_counts[k]
            elif op.signal:
                k = ("eng", op.eng)
                counters[op.eng] += 1
                op.sem = k
                op.val = counters[op.eng]
            else:
                continue
            if k not in keys:
                keys.append(k)
        with ExitStack() as es:
            sems = {}
            for i, k in enumerate(keys):
                sems[k] = es.enter_context(nc.semaphore("s%d" % i))
            block = es.enter_context(nc.Block())

            def make(engname):
                def body(e):
                    waited = {}
                    for op in self.ops:
                        if op.eng != engname:
                            continue
                        need = {}
                        for (o, kind) in op.deps.values():
                            if not self.needs_wait(o, op, kind):
                                continue
                            if need.get(o.sem, 0) < o.val:
                                need[o.sem] = o.val
                        for k, v in need.items():
                            if waited.get(k, 0) >= v:
                                continue
                            e.wait_ge(sems[k], v)
                            waited[k] = v
                        if op.fn is not None:
                            ins = op.fn(e)
                            if op.is_dma:
                                ins.then_inc(sems[op.sem], 16)
                            elif op.signal:
                                ins.then_inc(sems[op.sem], 1)
                return body

            block.tensor(make("pe"))
            block.scalar(make("act"))
            block.vector(make("dve"))
            block.gpsimd(make("pool"))
            block.sync(make("sp"))


class Arena:
    def __init__(self, base_ap, start=0):
        self.base = base_ap
        self.off = start

    def fork(self):
        return Arena(self.base, self.off)

    def get(self, dims, dt):
        n = int(np.prod(dims))
        if dt == F32:
            n *= 2
        a = self.base[:, self.off:self.off + n]
        self.off += (n + 15) // 16 * 16
        assert self.off <= self.base.shape[1], ("arena overflow", self.off)
        if dt == F32:
            a = a.bitcast(F32)
        if len(dims) == 2:
            a = a.rearrange("p (a b) -> p a b", a=dims[0])
        elif len(dims) == 3:
            a = a.rearrange("p (a b c) -> p a b c", a=dims[0], b=dims[1])
        elif len(dims) == 4:
            a = a.rearrange("p (a b c d) -> p a b c d", a=dims[0], b=dims[1], c=dims[2])
        return a


def bc(ap, shape):
    return ap.broadcast_to(list(shape))


def build_program(debug=False):
    nc = bass.Bass("TRN2", target_bir_lowering=False)

    def din(name, shape):
        return nc.dram_tensor(name, list(shape), F32, kind="ExternalInput").ap()

    x_own = din("x_own", [TOK, D])
    x_oth = din("x_oth", [TOK, D])
    w_in = din("w_in", [D, 4096])
    w_o_attn = din("w_o_attn", [512, D])
    w_o_pool = din("w_o_pool", [512, D])
    w_out = din("w_out", [D, D])
    w_ff1 = din("w_ff1", [D, DFF])
    w_ff2 = din("w_ff2", [DFF, D])
    w_pool = din("w_pool", [4, 128, 128])
    rel_table = din("rel_table", [32, 8])
    lnvec = din("lnvec", [5, D])
    bgate_l = din("bgate_l", [128, 16])
    bff1_l = din("bff1_l", [128, 32])
    pscale_l = din("pscale_l", [128, 4])
    ident_f = din("ident_f", [128, 128])
    jmat_f = din("jmat_f", [128, 128])
    oht_f = din("oht_f", [33, 384])
    gbias_f = din("gbias_f", [128, 512])
    rc0_f = din("rc0_f", [128, 64])
    out = nc.dram_tensor("out", [TOK, D], F32, kind="ExternalOutput").ap()
    bv_scr = nc.dram_tensor("bv_scr", [8, 384], F32, kind="Internal").ap()
    wff1_b = nc.dram_tensor("wff1_b", [128, 8 * DFF], BF16, kind="Internal").ap().rearrange("p (k n) -> p k n", k=8)
    wff2_b = nc.dram_tensor("wff2_b", [128, 32 * D], BF16, kind="Internal").ap().rearrange("p (f n) -> p f n", f=32)

    arena_t = nc.alloc_sbuf_tensor("arena", [128, 106000], BF16)
    AR = Arena(arena_t[:, :])
    ps = [nc.alloc_psum_tensor("ps%d" % b, [128, 512], F32) for b in range(8)]
    psb = [p.bitcast(BF16) for p in ps]

    P = Prog()
    uid = [0]

    def dkey(s):
        uid[0] += 1
        return "%s_%d" % (s, uid[0])

    ident = AR.get([128], BF16)
    aT = AR.get([4, TOK], BF16)

    w_in_v = w_in.rearrange("(kc p) n -> p kc n", p=128)

    A1 = AR.fork()
    jb = A1.get([128], BF16)
    E01 = A1.get([8, 256], BF16)
    cfar = A1.get([8], F32)
    gbias = A1.get([16, 32], F32)
    vm = A1.get([16, 32], F32)
    S0_START = 81000
    S0 = Arena(AR.base, S0_START)
    tabf = S0.get([8], F32)
    tab_hi = S0.get([8], BF16)
    tab_lo = S0.get([8], BF16)
    oht = S0.get([384], BF16)
    bvs = S0.get([384], F32)
    hk = S0.get([16, 128], F32)
    hk_hi = S0.get([16, 128], BF16)
    hk_lo = S0.get([16, 128], BF16)

    P.add("pool", lambda e: e.dma_start(out=ident, in_=ident_f), writes=["ident"], dma=dkey("c"))
    P.add("pool", lambda e: e.dma_start(out=jb, in_=jmat_f), writes=["jb"], dma=dkey("c"))
    P.add("pool", lambda e: e.dma_start(out=oht[0:33], in_=oht_f), writes=["oht"], dma=dkey("c"))
    P.add("sp", lambda e: e.dma_start(out=tabf[0:32], in_=rel_table), writes=["tabf_a"], dma=dkey("c"))
    P.add("dve", lambda e: e.memset(tabf[32:33], -30000.0), writes=["tabf_b"])
    P.add("sp", lambda e: e.dma_start(out=cfar, in_=bass.AP(rel_table.tensor, 31 * 8, [[0, 128], [1, 8]])),
          writes=["cfar"], dma=dkey("c"))
    P.add("sp", lambda e: e.dma_start(out=gbias.rearrange("p a b -> p (a b)"), in_=gbias_f),
          writes=["gbias"], dma=dkey("c"))
    P.add("dve", lambda e: e.tensor_single_scalar(out=vm, in_=gbias, scalar=-1.0, op=ALU.is_ge),
          reads=["gbias"], writes=["vm"])
    P.add("dve", lambda e: e.tensor_copy(out=tab_hi[0:33], in_=tabf[0:33]), reads=["tabf_a", "tabf_b"], writes=["tab_hi"])
    P.add("dve", lambda e: e.tensor_tensor(out=tab_lo[0:33], in0=tabf[0:33], in1=tab_hi[0:33], op=ALU.subtract),
          reads=["tabf_a", "tabf_b", "tab_hi"], writes=["tab_lo"])
    P.add("pe", lambda e: e.matmul(ps[0][0:8, 0:384], lhsT=tab_hi[0:33], rhs=oht[0:33], start=True, stop=False),
          reads=["tab_hi", "oht"], writes=["ps0"])
    P.add("pe", lambda e: e.matmul(ps[0][0:8, 0:384], lhsT=tab_lo[0:33], rhs=oht[0:33], start=False, stop=True),
          reads=["tab_lo", "oht"], writes=["ps0"])
    P.add("act", lambda e: e.activation(out=bvs[0:8], in_=ps[0][0:8, 0:384], func=AF.Copy), reads=["ps0"], writes=["bvs"])
    P.add("sp", lambda e: e.dma_start(out=bv_scr, in_=bvs[0:8]), reads=["bvs"], writes=["bv_scr"], dma=dkey("c"))
    for g in range(16):
        h, oi = g // 2, g % 2
        P.add("sp", (lambda g, h, oi: lambda e: e.dma_start(
            out=hk[:, g, :], in_=bass.AP(bv_scr.tensor, h * 384 + oi * 128, [[1, 128], [1, 128]])))(g, h, oi),
            reads=["bv_scr"], writes=["hk%d" % g], dma=dkey("hk"))
    hkr = ["hk%d" % g for g in range(16)]
    def setup_late():
        P.add("dve", lambda e: e.tensor_copy(out=hk_hi, in_=hk), reads=hkr, writes=["hk_hi"])
        P.add("dve", lambda e: e.tensor_tensor(out=hk_lo, in0=hk, in1=hk_hi, op=ALU.subtract), reads=hkr + ["hk_hi"], writes=["hk_lo"])
        for g in range(16):
            b, q = 1 + g // 4, g % 4
            P.add("pe", (lambda g, b, q: lambda e: e.matmul(ps[b][:, q * 128:(q + 1) * 128], lhsT=jb, rhs=hk_hi[:, g, :],
                                                           start=True, stop=False))(g, b, q),
                  reads=["jb", "hk_hi"], writes=["ps%d" % b])
            P.add("pe", (lambda g, b, q: lambda e: e.matmul(ps[b][:, q * 128:(q + 1) * 128], lhsT=jb, rhs=hk_lo[:, g, :],
                                                           start=False, stop=True))(g, b, q),
                  reads=["jb", "hk_lo"], writes=["ps%d" % b])
        for b4 in range(4):
            P.add("act", (lambda b4: lambda e: e.activation(
                out=E01[:, 2 * b4:2 * b4 + 2, :].rearrange("p a b -> p (a b)"), in_=ps[1 + b4][:, :], func=AF.Exp))(b4),
                reads=["ps%d" % (1 + b4)], writes=["E01"])
    SETUP_SCR = ["tabf_a", "tabf_b", "tab_hi", "tab_lo", "oht", "bvs", "hk_hi", "hk_lo"] + hkr

    A2 = A1.fork()
    wqkv = A2.get([8, 768], BF16)
    kto = A2.get([2, 16, 256], BF16)
    kown = A2.get([2, 16, 256], BF16)
    vo = A2.get([16, 2, 4, 65], BF16)
    vown = A2.get([16, 2, 4, 65], BF16)
    ksum = A2.get([2, 32], F32)
    kmean = A2.get([2, 32], BF16)
    xs = [A2.get([2, D], BF16) for _ in range(2)]
    xT = A2.get([8, 256], BF16)
    qT = A2.get([4, 256], BF16)
    gsb = A2.get([8, 32], F32)
    top8 = A2.get([8, 8], F32)
    msk = A2.get([8, 32], F32)
    PT = [A2.get([2, 256], BF16) for _ in range(4)]
    Oacc = A2.get([2, 4, 65], F32)
    rec = A2.get([2, 4], F32)
    atok = A2.get([2, 256], BF16)
    stg = [A2.get([8, 512], BF16) for _ in range(2)]
    PH2A_END = A2.off
    TAIL = Arena(AR.base, PH2A_END)
    wgp = TAIL.get([8, 2560], BF16)
    woa = TAIL.get([4, D], BF16)
    assert PH2A_END <= S0_START
    w_ff1_v = w_ff1.rearrange("(kc p) n -> p kc n", p=128)
    w_ff2_v = w_ff2.rearrange("(f p) n -> p f n", p=128)
    stg_ctr = [0]

    def stage_round():
        r = stg_ctr[0]
        if r >= 16:
            return
        stg_ctr[0] += 1
        sl = r % 2
        if r < 8:
            src = w_ff1_v[:, :, r * 512:(r + 1) * 512]
            dst = wff1_b[:, :, r * 512:(r + 1) * 512]
            sv = stg[sl]
            name = "wsc1_%d" % r
        else:
            src = w_ff2_v[:, (r - 8) * 4:(r - 7) * 4, :]
            dst = wff2_b[:, (r - 8) * 4:(r - 7) * 4, :]
            sv = stg[sl].rearrange("p a b -> p (a b)").rearrange("p (f n) -> p f n", f=4)
            name = "wsc2_%d" % (r - 8)
        P.add("pool", (lambda sv, src: lambda e: e.dma_start(out=sv, in_=src))(sv, src), writes=["stg%d" % sl], dma="stg%d" % sl)
        P.add("sp", (lambda sv, dst: lambda e: e.dma_start(out=dst, in_=sv))(sv, dst), reads=["stg%d" % sl], writes=[name],
              dma="stgst%d" % sl)

    first_wq = [True]
    xs_ctr = [0]
    pt_ctr = [0]
    s_ctr = [0]
    o_ctr = [0]
    SBANKS = [3, 5, 6]
    OBANKS = [4, 7]
    NPT = 4
    LAG = 2

    def load_block_xT(src, blk, banks=(0, 0), act_evac=True):
        slot = xs_ctr[0] % 2
        xs_ctr[0] += 1
        srcv = src[blk * 256:(blk + 1) * 256, :].rearrange("(t p) d -> p t d", p=128)
        P.add("pool", (lambda slot, srcv: lambda e: e.dma_start(out=xs[slot], in_=srcv))(slot, srcv),
              writes=["xs%d" % slot], dma="xs%d" % slot)
        for tt in range(2):
            tb = banks[tt]
            for kc in range(8):
                P.add("pe", (lambda slot, tt, kc, tb: lambda e: e.transpose(
                    out=psb[tb][:, kc * 128:(kc + 1) * 128], in_=xs[slot][:, tt, kc * 128:(kc + 1) * 128],
                    identity=ident))(slot, tt, kc, tb), reads=["xs%d" % slot, "ident"], writes=["ps%d" % tb])
            src_v = psb[tb][:, :].rearrange("p (k c) -> p k c", k=8)
            if tt == 0 or not act_evac:
                P.add("dve", (lambda tt, src_v: lambda e: e.tensor_copy(out=xT[:, :, tt * 128:(tt + 1) * 128], in_=src_v))(tt, src_v),
                      reads=["ps%d" % tb], writes=["xT%d" % tt])
            else:
                P.add("act", (lambda tt, src_v: lambda e: e.activation(out=xT[:, :, tt * 128:(tt + 1) * 128], in_=src_v, func=AF.Copy))(tt, src_v),
                      reads=["ps%d" % tb], writes=["xT%d" % tt])

    def proj_kv(kdst, vdst, j, kcol):
        for pl in range(2):
            for kc in range(8):
                P.add("pe", (lambda pl, kc: lambda e: e.matmul(
                    ps[1][:, pl * 256:(pl + 1) * 256], lhsT=wqkv[:, kc, 256 + pl * 128:256 + (pl + 1) * 128],
                    rhs=xT[:, kc, :], start=(kc == 0), stop=(kc == 7)))(pl, kc),
                    reads=["wqkv", "xT0", "xT1"], writes=["ps1"])
        for pl in range(2):
            P.add("act", (lambda pl: lambda e: e.activation(
                out=kdst[:, pl, j, :], in_=ps[1][:, pl * 256:(pl + 1) * 256], func=AF.Copy,
                accum_out=ksum[:, pl, kcol:kcol + 1]))(pl),
                reads=["ps1"], writes=["%s%d" % ("k", id(kdst)) + "_%d" % j, "ksum"])
        for tt in range(2):
            for kc in range(8):
                P.add("pe", (lambda tt, kc: lambda e: e.matmul(
                    ps[1][:, tt * 256:(tt + 1) * 256], lhsT=xT[:, kc, tt * 128:(tt + 1) * 128],
                    rhs=wqkv[:, kc, 512:768], start=(kc == 0), stop=(kc == 7)))(tt, kc),
                    reads=["wqkv", "xT0", "xT1"], writes=["ps1"])
        P.add("dve", lambda e: e.tensor_copy(
            out=vdst[:, j, :, :, 0:64], in_=ps[1][:, :].rearrange("p (t h d) -> p t h d", t=2, h=4)),
            reads=["ps1"], writes=["v%d_%d" % (id(vdst), j), "vones"])

    def kname(kdst, j):
        return "k%d_%d" % (id(kdst), j)

    def vname(vdst, j):
        return "v%d_%d" % (id(vdst), j)

    for hp in range(2):
        for part, c0 in enumerate((hp * 256, 512 + hp * 256, 1024 + hp * 256)):
            rd = []
            P.add("pool", (lambda part, c0: lambda e: e.dma_start(
                out=wqkv[:, :, part * 256:(part + 1) * 256], in_=w_in_v[:, :, c0:c0 + 256]))(part, c0),
                reads=[], writes=["wqkv"] + rd, dma=dkey("wqkv"))
            first_wq[0] = False
        P.add("dve", lambda e: e.memset(vo.rearrange("p a b c d -> p (a b c d)"), 1.0),
              writes=["vones"] + [vname(vo, j) for j in range(16)])
        P.add("dve", lambda e: e.memset(vown.rearrange("p a b c d -> p (a b c d)"), 1.0),
              writes=["vones"] + [vname(vown, j) for j in range(16)])
        P.add("dve", lambda e: e.memset(ksum, 0.0), writes=["ksum"])
        P.add("dve", lambda e: e.memset(kmean, 0.0), writes=["kmean"])
        P.add("dve", lambda e: e.memset(qT.rearrange("p a b -> p (a b)"), 0.0), writes=["qT"])

        for j in range(NBLK):
            load_block_xT(x_oth, j, banks=(0, 2))
            proj_kv(kto, vo, j, 16 + j)
        P.add("dve", lambda e: e.tensor_scalar_mul(out=kmean[:, :, 16:32], in0=ksum[:, :, 16:32], scalar1=1.0 / 256.0),
              reads=["ksum"], writes=["kmean"])

        if hp == 0:
            setup_late()
        if hp == 1:
            for q4 in range(4):
                P.add("pool", (lambda q4: lambda e: e.dma_start(
                    out=wgp[:, :, q4 * 640:(q4 + 1) * 640], in_=w_in_v[:, :, 1536 + q4 * 640:1536 + (q4 + 1) * 640]))(q4),
                    writes=["wgp%d" % q4] + SETUP_SCR, dma=dkey("wgp"))
            P.add("pool", lambda e: e.dma_start(out=woa, in_=w_o_attn.rearrange("(c p) n -> p c n", p=128)),
                  writes=["woa"] + SETUP_SCR, dma=dkey("woa"))
        for i in range(NBLK):
            load_block_xT(x_own, i, act_evac=False)
            stage_round()
            for pl in range(2):
                for kc in range(8):
                    P.add("pe", (lambda pl, kc: lambda e: e.matmul(
                        ps[2][:, pl * 256:(pl + 1) * 256], lhsT=wqkv[:, kc, pl * 128:(pl + 1) * 128],
                        rhs=xT[:, kc, :], start=(kc == 0), stop=(kc == 7)))(pl, kc),
                        reads=["wqkv", "xT0", "xT1"], writes=["ps2"])
            for hl in range(4):
                pl, eh = hl // 2, hl % 2
                P.add("dve", (lambda hl, pl, eh: lambda e: e.tensor_copy(
                    out=qT[eh * 64:(eh + 1) * 64, hl, :], in_=ps[2][eh * 64:(eh + 1) * 64, pl * 256:(pl + 1) * 256]))(hl, pl, eh),
                    reads=["ps2"], writes=["qT"])
            proj_kv(kown, vown, i, i)
            for hl in range(4):
                pl, eh = hl // 2, hl % 2
                for qt in range(2):
                    gidx = hl * 2 + qt
                    P.add("pe", (lambda pl, hl, qt, gidx: lambda e: e.matmul(
                        ps[0][:, gidx * 32:(gidx + 1) * 32],
                        lhsT=qT[:, hl, qt * 128:(qt + 1) * 128],
                        rhs=kmean[:, pl, :], start=True, stop=True))(pl, hl, qt, gidx),
                        reads=["qT", "kmean"], writes=["ps0"])
            P.add("dve", (lambda i: lambda e: e.tensor_tensor(
                out=gsb, in0=ps[0][:, 0:256].rearrange("p (a b) -> p a b", a=8),
                in1=bc(gbias[:, i:i + 1, :], [128, 8, 32]), op=ALU.add))(i),
                reads=["ps0", "gbias"], writes=["gsb"])
            for g in range(8):
                P.add("dve", (lambda g: lambda e: e.max(out=top8[:, g, :], in_=gsb[:, g, :]))(g),
                      reads=["gsb"], writes=["top8_%d" % g])
            P.add("dve", lambda e: e.tensor_tensor(out=msk, in0=gsb, in1=bc(top8[:, :, 2:3], [128, 8, 32]), op=ALU.is_ge),
                  reads=["gsb"] + ["top8_%d" % g for g in range(8)], writes=["msk"])
            P.add("dve", (lambda i: lambda e: e.tensor_tensor(out=msk, in0=msk, in1=bc(vm[:, i:i + 1, :], [128, 8, 32]),
                                                             op=ALU.mult))(i),
                  reads=["msk", "vm"], writes=["msk"])
            P.add("dve", (lambda i: lambda e: e.tensor_scalar_mul(out=kmean[:, :, i:i + 1], in0=ksum[:, :, i:i + 1],
                                                                 scalar1=1.0 / 256.0))(i),
                  reads=["ksum"], writes=["kmean"])

            visits = [("self", kown, vown, i, None)]
            visits += [("far", kown, vown, j, j) for j in range(i)]
            visits += [("far", kto, vo, j, 16 + j) for j in range(i)]
            visits += [("prev", kto, vo, i, 16 + i)]
            hvs = [(kind, kb, vb, j, col, hl) for (kind, kb, vb, j, col) in visits for hl in range(4)]

            def qk_exp(n):
                kind, kb, vb, j, col, hl = hvs[n]
                pl = hl // 2
                h = hp * 4 + hl
                sb = SBANKS[s_ctr[0] % len(SBANKS)]
                s_ctr[0] += 1
                pslot = pt_ctr[0] % NPT
                pt_ctr[0] += 1
                ptn = "PT%d" % pslot
                for kt in range(2):
                    P.add("pe", (lambda sb, kb, pl, hl, j, kt: lambda e: e.matmul(
                        ps[sb][:, kt * 256:(kt + 1) * 256],
                        lhsT=kb[:, pl, j, kt * 128:(kt + 1) * 128],
                        rhs=qT[:, hl, :], start=True, stop=True))(sb, kb, pl, hl, j, kt),
                        reads=[kname(kb, j), "qT"], writes=["ps%d" % sb])
                ptf = PT[pslot].rearrange("p a b -> p (a b)")
                if kind == "self":
                    P.add("act", (lambda sb, pslot: lambda e: e.activation(
                        out=PT[pslot][:, 0, :], in_=ps[sb][:, 0:256], func=AF.Exp, scale=0.125))(sb, pslot),
                        reads=["ps%d" % sb], writes=[ptn])
                    P.add("act", (lambda sb, pslot: lambda e: e.activation(
                        out=PT[pslot][:, 1, 128:256], in_=ps[sb][:, 384:512], func=AF.Exp, scale=0.125))(sb, pslot),
                        reads=["ps%d" % sb], writes=[ptn])
                    P.add("dve", (lambda pslot, h: lambda e: e.tensor_tensor(
                        out=PT[pslot][:, 0, :], in0=PT[pslot][:, 0, :], in1=E01[:, h, :], op=ALU.mult))(pslot, h),
                        reads=[ptn, "E01"], writes=[ptn])
                    P.add("dve", (lambda pslot, h: lambda e: e.tensor_tensor(
                        out=PT[pslot][:, 1, 128:256], in0=PT[pslot][:, 1, 128:256], in1=E01[:, h, 0:128],
                        op=ALU.mult))(pslot, h),
                        reads=[ptn, "E01"], writes=[ptn])
                elif kind == "far":
                    P.add("act", (lambda sb, ptf, h: lambda e: e.activation(
                        out=ptf, in_=ps[sb][:, :], func=AF.Exp, scale=0.125, bias=cfar[:, h:h + 1]))(sb, ptf, h),
                        reads=["ps%d" % sb, "cfar"], writes=[ptn])
                else:
                    P.add("act", (lambda sb, pslot, h: lambda e: e.activation(
                        out=PT[pslot][:, 0, :], in_=ps[sb][:, 0:256], func=AF.Exp, scale=0.125,
                        bias=cfar[:, h:h + 1]))(sb, pslot, h),
                        reads=["ps%d" % sb, "cfar"], writes=[ptn])
                    P.add("act", (lambda sb, pslot, h: lambda e: e.activation(
                        out=PT[pslot][:, 1, 128:256], in_=ps[sb][:, 384:512], func=AF.Exp, scale=0.125,
                        bias=cfar[:, h:h + 1]))(sb, pslot, h),
                        reads=["ps%d" % sb, "cfar"], writes=[ptn])
                    if kind == "prev":
                        P.add("act", (lambda sb, pslot: lambda e: e.activation(
                            out=PT[pslot][:, 1, 0:128], in_=ps[sb][:, 256:384], func=AF.Exp, scale=0.125))(sb, pslot),
                            reads=["ps%d" % sb], writes=[ptn])
                        P.add("dve", (lambda pslot, h: lambda e: e.tensor_tensor(
                            out=PT[pslot][:, 1, 0:128], in0=PT[pslot][:, 1, 0:128], in1=E01[:, h, 128:256],
                            op=ALU.mult))(pslot, h),
                            reads=[ptn, "E01"], writes=[ptn])
                return pslot

            def pv_acc(n, pslot):
                kind, kb, vb, j, col, hl = hvs[n]
                ptn = "PT%d" % pslot
                ob = OBANKS[o_ctr[0] % len(OBANKS)]
                o_ctr[0] += 1
                on = "ps%d" % ob
                for qt in range(2):
                    kts = [0] if (kind == "self" and qt == 0) else [0, 1]
                    for kt in kts:
                        P.add("pe", (lambda ob, qt, kt, pslot, vb, j, hl, kts: lambda e: e.matmul(
                            ps[ob][:, qt * 65:(qt + 1) * 65],
                            lhsT=PT[pslot][:, kt, qt * 128:(qt + 1) * 128],
                            rhs=vb[:, j, kt, hl, :], start=(kt == kts[0]), stop=(kt == kts[-1])))(
                                ob, qt, kt, pslot, vb, j, hl, kts),
                            reads=[ptn, vname(vb, j), "vones"], writes=[on])
                if kind == "self":
                    P.add("dve", (lambda ob, hl: lambda e: e.tensor_copy(
                        out=Oacc[:, :, hl, :], in_=ps[ob][:, 0:130].rearrange("p (a b) -> p a b", a=2)))(ob, hl),
                        reads=[on], writes=["Oacc%d" % hl])
                else:
                    for qt in range(2):
                        P.add("dve", (lambda ob, hl, qt, col: lambda e: e.scalar_tensor_tensor(
                            out=Oacc[:, qt, hl, :], in0=ps[ob][:, qt * 65:(qt + 1) * 65],
                            scalar=msk[:, hl * 2 + qt, col:col + 1], in1=Oacc[:, qt, hl, :],
                            op0=ALU.mult, op1=ALU.add))(ob, hl, qt, col),
                            reads=[on, "msk", "Oacc%d" % hl], writes=["Oacc%d" % hl])

            slots = {}
            for n in range(len(hvs)):
                slots[n] = qk_exp(n)
                if n >= LAG:
                    pv_acc(n - LAG, slots.pop(n - LAG))
            for n in range(max(0, len(hvs) - LAG), len(hvs)):
                pv_acc(n, slots.pop(n))
            oar = ["Oacc%d" % hl for hl in range(4)]
            P.add("dve", lambda e: e.reciprocal(out=rec, in_=Oacc[:, :, :, 64]), reads=oar, writes=["rec"])
            P.add("dve", lambda e: e.tensor_tensor(
                out=atok.rearrange("p a (h d) -> p a h d", h=4), in0=Oacc[:, :, :, 0:64],
                in1=bc(rec.unsqueeze(3), [128, 2, 4, 64]), op=ALU.mult),
                reads=oar + ["rec"], writes=["atok"])
            for c in range(2):
                for qt in range(2):
                    P.add("pe", (lambda c, qt: lambda e: e.transpose(
                        out=psb[0][:, (c * 2 + qt) * 128:(c * 2 + qt + 1) * 128],
                        in_=atok[:, qt, c * 128:(c + 1) * 128], identity=ident))(c, qt),
                        reads=["atok", "ident"], writes=["ps0"])
            P.add("dve", (lambda hp, i: lambda e: e.tensor_copy(
                out=aT[:, hp * 2:hp * 2 + 2, i * 256:(i + 1) * 256],
                in_=psb[0][:, 0:512].rearrange("p (c q) -> p c q", c=2)))(hp, i),
                reads=["ps0"], writes=["aT%d" % i])

    PH2A = ["wqkv", "ksum", "kmean", "xs0", "xs1", "xT0", "xT1", "qT", "gsb", "msk", "PT0", "PT1", "PT2", "PT3",
            "Oacc0", "Oacc1", "Oacc2", "Oacc3", "rec", "atok", "stg0", "stg1", "E01", "cfar", "gbias", "vm", "vones"] + \
           ["top8_%d" % g for g in range(8)] + \
           [kname(kto, j) for j in range(16)] + [kname(kown, j) for j in range(16)] + \
           [vname(vo, j) for j in range(16)] + [vname(vown, j) for j in range(16)]

    B = AR.fork()
    wop = B.get([4, D], BF16)
    wout = B.get([8, D], BF16)
    wpl = B.get([4, 128], BF16)
    lnc = B.get([5, D], F32)
    bgate = B.get([16], F32)
    bff1 = B.get([32], F32)
    pscale = B.get([4], F32)
    rc0 = B.get([4, 16], F32)
    xs2 = B.get([2, D], BF16)
    xh = B.get([D], BF16)
    xT2 = B.get([8, 272], BF16)
    pf = B.get([4, 272], F32)
    wA = B.get([4, 272], F32)
    wB = B.get([4, 272], F32)
    pooled = B.get([4, 256], BF16)
    ptmp = B.get([16], F32)
    BT = B.get([4, 256], BF16)
    sg = [B.get([2, 256], F32) for _ in range(2)]
    t12 = sg
    uT = B.get([8, 256], BF16)
    xf = B.get([2, D], F32)
    hb = xs2
    hT = B.get([8, 256], BF16)
    stats = B.get([2, 2, 6], F32)
    mv = B.get([2, 2], F32)
    rstd = B.get([2], F32)
    rr = [B.get([256], F32) for _ in range(2)]
    NWS = 3
    rT = [B.get([FG, 256], BF16) for _ in range(NWS)]
    w1c = [B.get([8, FG * 128], BF16) for _ in range(NWS)]
    w2c = [B.get([FG, D], BF16) for _ in range(NWS)]

    assert B.off <= PH2A_END and TAIL.off <= 106000
    first2b = [True]

    def wdma(dst, src, name, eng="pool"):
        extra = (PH2A + ["bar2b"]) if first2b[0] else []
        rd = [] if first2b[0] else ["bar2b"]
        first2b[0] = False
        P.add(eng, (lambda dst, src: lambda e: e.dma_start(out=dst, in_=src))(dst, src),
              reads=rd, writes=[name] + extra, dma=dkey(name))

    wdma(bgate, bgate_l, "bgate", eng="sp")
    WGP = ["wgp%d" % q for q in range(4)]
    wdma(wop, w_o_pool.rearrange("(c p) n -> p c n", p=128), "wop")
    wdma(wout, w_out.rearrange("(c p) n -> p c n", p=128), "wout")
    wdma(wpl, w_pool.rearrange("g c d -> c g d"), "wpl")
    for k in range(5):
        P.add("sp", (lambda k: lambda e: e.dma_start(out=lnc[:, k, :], in_=bass.AP(lnvec.tensor, k * D, [[0, 128], [1, D]])))(k),
              reads=["bar2b"], writes=["lnc%d" % k], dma=dkey("lnc"))
    wdma(bff1, bff1_l, "bff1", eng="sp")
    wdma(pscale, pscale_l, "pscale", eng="sp")
    wdma(rc0.rearrange("p a b -> p (a b)"), rc0_f, "rc0", eng="sp")

    def layer_norm(gk, bk):
        for tt in range(2):
            for hf in range(2):
                P.add("dve", (lambda tt, hf: lambda e: e.bn_stats(out=stats[:, tt, hf, :], in_=xf[:, tt, hf * 512:(hf + 1) * 512]))(tt, hf),
                      reads=["xf%d" % tt], writes=["stats%d%d" % (tt, hf)])
            P.add("dve", (lambda tt: lambda e: e.bn_aggr(out=mv[:, tt, :], in_=stats[:, tt, :, :]))(tt),
                  reads=["stats%d0" % tt, "stats%d1" % tt], writes=["mv%d" % tt])
        P.add("dve", lambda e: e.tensor_scalar_add(out=rstd, in0=mv[:, :, 1], scalar1=EPS), reads=["mv0", "mv1"], writes=["rstd"])
        P.add("act", lambda e: e.activation(out=rstd, in_=rstd, func=AF.Sqrt), reads=["rstd"], writes=["rstd"])
        P.add("dve", lambda e: e.reciprocal(out=rstd, in_=rstd), reads=["rstd"], writes=["rstd"])
        for tt in range(2):
            P.add("dve", (lambda tt: lambda e: e.scalar_tensor_tensor(
                out=xf[:, tt, :], in0=xf[:, tt, :], scalar=mv[:, tt, 0:1], in1=lnc[:, gk, :],
                op0=ALU.subtract, op1=ALU.mult))(tt),
                reads=["xf%d" % tt, "mv%d" % tt, "lnc%d" % gk], writes=["xf%d" % tt])
            P.add("dve", (lambda tt: lambda e: e.scalar_tensor_tensor(
                out=xf[:, tt, :], in0=xf[:, tt, :], scalar=rstd[:, tt:tt + 1], in1=lnc[:, bk, :],
                op0=ALU.mult, op1=ALU.add))(tt),
                reads=["xf%d" % tt, "rstd", "lnc%d" % bk], writes=["xf%d" % tt])

    wslot = [0]
    fbank = [0]
    XT2 = ["xT2h", "xT2_0", "xT2_1"]
    PF = ["pf%d" % g for g in range(4)]

    def own_view(i):
        return x_own[i * 256:(i + 1) * 256, :].rearrange("(t p) d -> p t d", p=128)

    def X_load(i):
        P.add("pool", (lambda v: lambda e: e.dma_start(out=xs2, in_=v))(own_view(i)), reads=["bar2b"], writes=["xs2"], dma="xs2")
        P.add("pool", (lambda i: lambda e: e.dma_start(out=xh[0:16], in_=x_oth[i * 256 + 240:(i + 1) * 256, :]))(i),
              reads=["bar2b"], writes=["xh"], dma="xh")

    def XF_load(i):
        P.add("sp", (lambda v: lambda e: e.dma_start(out=xf, in_=v))(own_view(i)), reads=["bar2b"], writes=["xf0", "xf1"], dma="xf")

    def X(i):
        for kc in range(8):
            P.add("pe", (lambda kc: lambda e: e.transpose(out=psb[0][:, kc * 16:(kc + 1) * 16], in_=xh[0:16, kc * 128:(kc + 1) * 128],
                                                         identity=ident[0:16, 0:16]))(kc),
                  reads=["xh", "ident"], writes=["ps0"])
        P.add("act", lambda e: e.activation(out=xT2[:, :, 0:16], in_=psb[0][:, 0:128].rearrange("p (k c) -> p k c", k=8), func=AF.Copy),
              reads=["ps0"], writes=["xT2h"])
        for tt in range(2):
            for kc in range(8):
                P.add("pe", (lambda tt, kc: lambda e: e.transpose(
                    out=psb[0][:, kc * 128:(kc + 1) * 128], in_=xs2[:, tt, kc * 128:(kc + 1) * 128], identity=ident))(tt, kc),
                    reads=["xs2", "ident"], writes=["ps0"])
            P.add("act", (lambda tt: (lambda e: e.activation(
                out=xT2[:, :, 16 + tt * 128:16 + (tt + 1) * 128], in_=psb[0][:, :].rearrange("p (k c) -> p k c", k=8), func=AF.Copy))
                if True else None)(tt),
                reads=["ps0"], writes=["xT2_%d" % tt])

    def Pm_a(i):
        for g in range(4):
            pb = 1 + (g % 2)
            for kc in range(8):
                P.add("pe", (lambda g, pb, kc: lambda e: e.matmul(
                    ps[pb][:, 0:272], lhsT=wgp[:, kc, g * 128:(g + 1) * 128], rhs=xT2[:, kc, :],
                    start=(kc == 0), stop=(kc == 7)))(g, pb, kc),
                    reads=XT2 + WGP, writes=["ps%d" % pb])
            P.add("act", (lambda g, pb: lambda e: e.activation(out=pf[:, g, :], in_=ps[pb][:, 0:272], func=AF.Copy))(g, pb),
                  reads=["ps%d" % pb], writes=["pf%d" % g])

    def Pm_b(i):
        P.add("dve", lambda e: e.tensor_tensor(out=wA[:, :, 1:272], in0=pf[:, :, 1:272], in1=pf[:, :, 0:271], op=ALU.add),
              reads=PF, writes=["wA"])
        P.add("dve", lambda e: e.tensor_tensor(out=wB[:, 1:4, 3:272], in0=wA[:, 1:4, 3:272], in1=wA[:, 1:4, 1:270], op=ALU.add),
              reads=["wA"], writes=["wB"])
        P.add("dve", lambda e: e.tensor_tensor(out=wA[:, 2:4, 7:272], in0=wB[:, 2:4, 7:272], in1=wB[:, 2:4, 3:268], op=ALU.add),
              reads=["wB"], writes=["wA"])
        P.add("dve", lambda e: e.tensor_tensor(out=wB[:, 3, 15:272], in0=wA[:, 3, 15:272], in1=wA[:, 3, 7:264], op=ALU.add),
              reads=["wA"], writes=["wB"])
        for g in range(4):
            fin_ = wA if g % 2 == 0 else wB
            P.add("dve", (lambda g, fin_: lambda e: e.scalar_tensor_tensor(
                out=pooled[:, g, :], in0=fin_[:, g, 16:272], scalar=1.0 / WINS[g], in1=pf[:, g, 16:272],
                op0=ALU.mult, op1=ALU.subtract))(g, fin_),
                reads=["wA", "wB"] + PF, writes=["pooled%d" % g])
            if i == 0:
                P.add("dve", (lambda g, fin_: lambda e: e.tensor_tensor(out=ptmp, in0=fin_[:, g, 16:32], in1=rc0[:, g, :], op=ALU.mult))(g, fin_),
                      reads=["wA", "wB", "rc0"], writes=["ptmp"])
                P.add("dve", (lambda g: lambda e: e.tensor_tensor(out=pooled[:, g, 0:16], in0=ptmp, in1=pf[:, g, 16:32], op=ALU.subtract))(g),
                      reads=["ptmp"] + PF, writes=["pooled%d" % g])
        for g in range(4):
            pb = 1 + (g // 2)
            P.add("pe", (lambda g, pb: lambda e: e.matmul(
                ps[pb][:, (g % 2) * 256:(g % 2 + 1) * 256], lhsT=wpl[:, g, :], rhs=pooled[:, g, :], start=True, stop=True))(g, pb),
                reads=["wpl", "pooled%d" % g], writes=["ps%d" % pb])
        for g in range(4):
            pb = 1 + (g // 2)
            P.add("dve", (lambda g, pb: lambda e: e.tensor_scalar_mul(
                out=BT[:, g, :], in0=ps[pb][:, (g % 2) * 256:(g % 2 + 1) * 256], scalar1=pscale[:, g:g + 1]))(g, pb),
                reads=["ps%d" % pb, "pscale"], writes=["BT"])

    def G(i):
        for dc in range(8):
            s = dc % 2
            ba, bb = 1 + 2 * s, 2 + 2 * s
            for kc in range(8):
                P.add("pe", (lambda ba, dc, kc: lambda e: e.matmul(
                    ps[ba][:, 0:256], lhsT=wgp[:, kc, 512 + dc * 128:512 + (dc + 1) * 128], rhs=xT2[:, kc, 16:272],
                    start=(kc == 0), stop=(kc == 7)))(ba, dc, kc), reads=XT2 + WGP, writes=["ps%d" % ba])
            for kc in range(8):
                P.add("pe", (lambda ba, dc, kc: lambda e: e.matmul(
                    ps[ba][:, 256:512], lhsT=wgp[:, kc, 1536 + dc * 128:1536 + (dc + 1) * 128], rhs=xT2[:, kc, 16:272],
                    start=(kc == 0), stop=(kc == 7)))(ba, dc, kc), reads=XT2 + WGP, writes=["ps%d" % ba])
            for c in range(4):
                P.add("pe", (lambda bb, dc, c, i: lambda e: e.matmul(
                    ps[bb][:, 0:256], lhsT=woa[:, c, dc * 128:(dc + 1) * 128], rhs=aT[:, c, i * 256:(i + 1) * 256],
                    start=(c == 0), stop=(c == 3)))(bb, dc, c, i), reads=["woa", "aT%d" % i], writes=["ps%d" % bb])
            for c in range(4):
                P.add("pe", (lambda bb, dc, c: lambda e: e.matmul(
                    ps[bb][:, 256:512], lhsT=wop[:, c, dc * 128:(dc + 1) * 128], rhs=BT[:, c, :],
                    start=(c == 0), stop=(c == 3)))(bb, dc, c), reads=["wop", "BT"], writes=["ps%d" % bb])
            for ab in range(2):
                P.add("act", (lambda s, ba, ab, dc: lambda e: e.activation(
                    out=sg[s][:, ab, :], in_=ps[ba][:, ab * 256:(ab + 1) * 256], func=AF.Sigmoid,
                    bias=bgate[:, ab * 8 + dc:ab * 8 + dc + 1]))(s, ba, ab, dc),
                    reads=["ps%d" % ba, "bgate"], writes=["sg%d" % s])
            P.add("dve", (lambda s, bb: lambda e: e.tensor_tensor(
                out=t12[s], in0=sg[s], in1=ps[bb][:, :].rearrange("p (a b) -> p a b", a=2), op=ALU.mult))(s, bb),
                reads=["sg%d" % s, "ps%d" % bb], writes=["sg%d" % s])
            P.add("dve", (lambda s, dc: lambda e: e.tensor_tensor(out=uT[:, dc, :], in0=t12[s][:, 0, :], in1=t12[s][:, 1, :], op=ALU.add))(s, dc),
                  reads=["sg%d" % s], writes=["uT"])

    def Mx_mm(i):
        for tt in range(2):
            for hf in range(2):
                pb = 4 + tt * 2 + hf
                for dc in range(8):
                    P.add("pe", (lambda pb, tt, hf, dc: lambda e: e.matmul(
                        ps[pb][:, :], lhsT=uT[:, dc, tt * 128:(tt + 1) * 128], rhs=wout[:, dc, hf * 512:(hf + 1) * 512],
                        start=(dc == 0), stop=(dc == 7)))(pb, tt, hf, dc), reads=["uT", "wout"], writes=["ps%d" % pb])

    def Mx_stt(i):
        for tt in range(2):
            for hf in range(2):
                pb = 4 + tt * 2 + hf
                P.add("dve", (lambda pb, tt, hf: lambda e: e.scalar_tensor_tensor(
                    out=xf[:, tt, hf * 512:(hf + 1) * 512], in0=xf[:, tt, hf * 512:(hf + 1) * 512], scalar=ALPHA,
                    in1=ps[pb][:, :], op0=ALU.mult, op1=ALU.add))(pb, tt, hf),
                    reads=["xf%d" % tt, "ps%d" % pb], writes=["xf%d" % tt])

    def L1(i):
        layer_norm(0, 1)
        for tt in range(2):
            P.add("act", (lambda tt: lambda e: e.activation(out=hb[:, tt, :], in_=xf[:, tt, :], func=AF.Copy))(tt),
                  reads=["xf%d" % tt], writes=["xs2"])
            for kc in range(8):
                P.add("pe", (lambda tt, kc: lambda e: e.transpose(
                    out=psb[0][:, kc * 128:(kc + 1) * 128], in_=hb[:, tt, kc * 128:(kc + 1) * 128], identity=ident))(tt, kc),
                    reads=["xs2", "ident"], writes=["ps0"])
            P.add("dve", (lambda tt: lambda e: e.tensor_copy(
                out=hT[:, :, tt * 128:(tt + 1) * 128], in_=psb[0][:, :].rearrange("p (k c) -> p k c", k=8)))(tt),
                reads=["ps0"], writes=["hT%d" % tt])

    def ff2(gi, ws):
        for tt in range(2):
            for hf in range(2):
                pb = 4 + tt * 2 + hf
                for f in range(FG):
                    P.add("pe", (lambda pb, tt, hf, f, ws, gi: lambda e: e.matmul(
                        ps[pb][:, :], lhsT=rT[ws][:, f, tt * 128:(tt + 1) * 128], rhs=w2c[ws][:, f, hf * 512:(hf + 1) * 512],
                        start=(gi == 0 and f == 0), stop=(gi == NFG - 1 and f == FG - 1)))(pb, tt, hf, f, ws, gi),
                        reads=["rT%d" % ws, "w2c%d" % ws], writes=["ps%d" % pb])

    def Fw(gi):
        ws = wslot[0] % NWS
        wslot[0] += 1
        c0 = gi * FG * 128
        P.add("sp", (lambda ws, c0: lambda e: e.dma_start(out=w1c[ws], in_=wff1_b[:, :, c0:c0 + FG * 128]))(ws, c0),
              reads=["bar2b", "wsc1_%d" % (c0 // 512)], writes=["w1c%d" % ws], dma="w1c%d" % ws)
        P.add("sp", (lambda ws, gi: lambda e: e.dma_start(out=w2c[ws], in_=wff2_b[:, gi * FG:(gi + 1) * FG, :]))(ws, gi),
              reads=["bar2b", "wsc2_%d" % (gi * FG // 4)], writes=["w2c%d" % ws], dma="w2c%d" % ws)
        return ws

    def F(i):
        pend = None
        for gi in range(NFG):
            ws = Fw(gi)
            for f in range(FG):
                fc = gi * FG + f
                fb = 1 + (fbank[0] % 3)
                fbank[0] += 1
                for kc in range(8):
                    P.add("pe", (lambda fb, ws, f, kc: lambda e: e.matmul(
                        ps[fb][:, 0:256], lhsT=w1c[ws][:, kc, f * 128:(f + 1) * 128], rhs=hT[:, kc, :],
                        start=(kc == 0), stop=(kc == 7)))(fb, ws, f, kc),
                        reads=["w1c%d" % ws, "hT0", "hT1"], writes=["ps%d" % fb])
                rs = fc % 2
                P.add("act", (lambda fb, rs, fc: lambda e: e.activation(
                    out=rr[rs], in_=ps[fb][:, 0:256], func=AF.Relu, bias=bff1[:, fc:fc + 1]))(fb, rs, fc),
                    reads=["ps%d" % fb, "bff1"], writes=["rr%d" % rs])
                P.add("act", (lambda rs, ws, f: lambda e: e.activation(out=rT[ws][:, f, :], in_=rr[rs], func=AF.Square))(rs, ws, f),
                      reads=["rr%d" % rs], writes=["rT%d" % ws])
            if pend is not None:
                ff2(*pend)
            pend = (gi, ws)
        ff2(*pend)

    def L2a(i):
        for tt in range(2):
            for hf in range(2):
                pb = 4 + tt * 2 + hf
                P.add("dve", (lambda pb, tt, hf: lambda e: e.scalar_tensor_tensor(
                    out=xf[:, tt, hf * 512:(hf + 1) * 512], in0=xf[:, tt, hf * 512:(hf + 1) * 512], scalar=ALPHA,
                    in1=ps[pb][:, :], op0=ALU.mult, op1=ALU.add))(pb, tt, hf),
                    reads=["xf%d" % tt, "ps%d" % pb], writes=["xf%d" % tt])

    def L2b(i):
        for tt in range(2):
            P.add("dve", (lambda tt: lambda e: e.tensor_tensor(out=xf[:, tt, :], in0=xf[:, tt, :], in1=lnc[:, 4, :], op=ALU.add))(tt),
                  reads=["xf%d" % tt, "lnc4"], writes=["xf%d" % tt])
        layer_norm(2, 3)
        out_v = out[i * 256:(i + 1) * 256, :].rearrange("(t p) d -> p t d", p=128)
        P.add("sp", (lambda out_v: lambda e: e.dma_start(out=out_v, in_=xf))(out_v), reads=["xf0", "xf1"],
              writes=["out%d" % i], dma="outst")

    X_load(0)
    XF_load(0)
    X(0)
    Pm_a(0)
    Pm_b(0)
    G(0)
    X_load(1)
    for i in range(NBLK):
        nxt = i + 1 < NBLK
        Mx_mm(i)
        if nxt:
            X(i + 1)
            Pm_a(i + 1)
        Mx_stt(i)
        L1(i)
        if i + 2 < NBLK:
            X_load(i + 2)
        if nxt:
            Pm_b(i + 1)
        F(i)
        L2a(i)
        if nxt:
            G(i + 1)
        L2b(i)
        if nxt:
            XF_load(i + 1)
    fin = ["out%d" % i for i in range(NBLK)]
    if debug:
        dbg_aT = nc.dram_tensor("dbg_aT", [128, 4 * TOK], BF16, kind="ExternalOutput").ap()
        P.add("sp", lambda e: e.dma_start(out=dbg_aT, in_=aT.rearrange("p a b -> p (a b)")),
              reads=["aT%d" % i for i in range(NBLK)], writes=["dbg_aT"], dma="dbg_aT")
        fin.append("dbg_aT")
    P.add("sp", None, reads=fin)
    P.emit(nc)
    return nc


def _rel_bucket_np(dist):
    n = np.maximum(dist, 0)
    nf = np.maximum(n, 1).astype(np.float32)
    large = 16 + (np.log(nf / np.float32(16)) / np.float32(math.log(128 / 16)) * np.float32(16)).astype(np.int32)
    large = np.minimum(large, 31)
    return np.where(n < 16, n, large)


_CACHE = {}


def _consts():
    if "c" in _CACHE:
        return _CACHE["c"]
    ident = np.eye(128, dtype=np.float32)
    jmat = np.ascontiguousarray(ident[::-1])
    dist = np.arange(384) - 127
    bkt = _rel_bucket_np(dist.astype(np.int32))
    oht = np.zeros((33, 384), np.float32)
    for d in range(384):
        if dist[d] < 0:
            oht[32, d] = 1.0
        else:
            oht[bkt[d], d] = 1.0
    gb = np.zeros((2, 16, 32), np.float32)
    for par in range(2):
        for i in range(16):
            for c in range(32):
                valid = (c < i) if c < 16 else ((c - 16) <= i and not (par == 0 and c == 16))
                gb[par, i, c] = 0.0 if valid else NEG
    rc = np.zeros((2, 4, 16), np.float32)
    for par in range(2):
        for g, w in enumerate(WINS):
            for t in range(16):
                rc[par, g, t] = 1.0 / (min(t + 1, w) if par == 0 else w)
    c = dict(ident=ident, jmat=jmat, oht=oht,
             gbias=[np.ascontiguousarray(np.broadcast_to(gb[p].reshape(1, 512), (128, 512))) for p in range(2)],
             rc0=[np.ascontiguousarray(np.broadcast_to(rc[p].reshape(1, 64), (128, 64))) for p in range(2)])
    _CACHE["c"] = c
    return c


def kernel(x, w_in, b_gate, rel_table, w_pool, pool_scale, w_o_attn, w_o_pool, w_out,
           ln1_g, ln1_b, w_ff1, b_ff1, w_ff2, b_ff2, ln2_g, ln2_b):
    f = lambda a: np.ascontiguousarray(np.asarray(a, dtype=np.float32))
    x = f(x)
    c = _consts()
    if "nc" not in _CACHE:
        _CACHE["nc"] = build_program()
    nc = _CACHE["nc"]
    shared = dict(
        w_in=f(w_in[0]), w_o_attn=f(w_o_attn[0]), w_o_pool=f(w_o_pool[0]), w_out=f(w_out[0]),
        w_ff1=f(w_ff1[0]), w_ff2=f(w_ff2[0]), w_pool=f(w_pool[0]), rel_table=f(rel_table),
        lnvec=f(np.stack([np.asarray(ln1_g)[0], np.asarray(ln1_b)[0], np.asarray(ln2_g)[0], np.asarray(ln2_b)[0],
                          np.asarray(b_ff2)[0]])),
        bgate_l=f(np.asarray(b_gate)[0].reshape(16, 128).T), bff1_l=f(np.asarray(b_ff1)[0].reshape(32, 128).T),
        pscale_l=f(np.asarray(pool_scale)[0].reshape(4, 128).T),
        ident_f=c["ident"], jmat_f=c["jmat"], oht_f=c["oht"])
    in_maps = []
    for core in range(8):
        b, par = core // 2, core % 2
        xb = x[b].reshape(32, 256, D)
        own = xb[par::2]
        if par == 0:
            oth = np.concatenate([np.zeros((1, 256, D), np.float32), xb[1::2][:15]], axis=0)
        else:
            oth = xb[0::2]
        m = dict(shared)
        m["x_own"] = np.ascontiguousarray(own.reshape(TOK, D))
        m["x_oth"] = np.ascontiguousarray(oth.reshape(TOK, D))
        m["gbias_f"] = c["gbias"][par]
        m["rc0_f"] = c["rc0"][par]
        in_maps.append(m)
    res = run_bass_kernel_spmd(nc, in_maps, core_ids=list(range(8)))
    outp = np.empty((4, 32, 256, D), np.float32)
    for core in range(8):
        b, par = core // 2, core % 2
        outp[b, par::2] = np.asarray(res.results[core]["out"], dtype=np.float32).reshape(16, 256, D)
    return outp.reshape(4, 8192, D)
```

```python
from contextlib import ExitStack
import math

import numpy as np
import concourse.bass as bass
import concourse.mybir as mybir
from concourse.bass_utils import run_bass_kernel_spmd

F32 = mybir.dt.float32
BF16 = mybir.dt.bfloat16
AF = mybir.ActivationFunctionType
ALU = mybir.AluOpType

D = 1024
NBLK = 16
BL = 256
TOK = NBLK * BL
DFF = 4096
ALPHA = 2.0 ** 0.25
EPS = 1e-5
NEG = -1.0e30
WINS = (2, 4, 8, 16)
FG = 2
NFG = 32 // FG


class _Op:
    __slots__ = ("eng", "fn", "deps", "is_dma", "semkey", "signal", "sem", "val", "idx")


class Prog:
    ENGS = ("pe", "act", "dve", "pool", "sp")

    def __init__(self):
        self.ops = []
        self.last_w = {}
        self.readers = {}

    def add(self, eng, fn, reads=(), writes=(), dma=None):
        op = _Op()
        op.eng = eng
        op.fn = fn
        op.is_dma = dma is not None
        op.semkey = dma
        op.deps = {}
        op.signal = False
        op.idx = len(self.ops)
        op.sem = None
        op.val = 0
        writes = list(writes) + [r for r in reads if r.startswith("ps") and r not in writes]
        reads = [r for r in reads if not r.startswith("ps")]
        for r in reads:
            w = self.last_w.get(r)
            if w is not None:
                op.deps[w.idx] = (w, "raw")
        for r in writes:
            w = self.last_w.get(r)
            if w is not None and w.idx not in op.deps:
                op.deps[w.idx] = (w, "waw")
            for o in self.readers.get(r, {}).values():
                if o.idx not in op.deps:
                    op.deps[o.idx] = (o, "war")
        for r in reads:
            d = self.readers.setdefault(r, {})
            d[("dma", op.idx) if op.is_dma else eng] = op
        for r in writes:
            self.last_w[r] = op
            self.readers[r] = {}
        self.ops.append(op)
        return op

    @staticmethod
    def needs_wait(o, op, kind):
        if o.is_dma or op.is_dma:
            return True
        if o.eng != op.eng:
            return True
        return op.eng != "pe"

    def emit(self, nc):
        for op in self.ops:
            for (o, kind) in op.deps.values():
                if self.needs_wait(o, op, kind):
                    o.signal = True
        counters = {e: 0 for e in self.ENGS}
        dma_counts = {}
        keys = []
        for op in self.ops:
            if op.is_dma:
                k = ("dma", op.semkey)
                dma_counts[k] = dma_counts.get(k, 0) + 16
                op.sem = k
                op.val = dma_counts[k]
            elif op.signal:
                k = ("eng", op.eng)
                counters[op.eng] += 1
                op.sem = k
                op.val = counters[op.eng]
            else:
                continue
            if k not in keys:
                keys.append(k)
        with ExitStack() as es:
            sems = {}
            for i, k in enumerate(keys):
                sems[k] = es.enter_context(nc.semaphore("s%d" % i))
            block = es.enter_context(nc.Block())

            def make(engname):
                def body(e):
                    waited = {}
                    for op in self.ops:
                        if op.eng != engname:
                            continue
                        need = {}
                        for (o, kind) in op.deps.values():
                            if not self.needs_wait(o, op, kind):
                                continue
                            if need.get(o.sem, 0) < o.val:
                                need[o.sem] = o.val
                        for k, v in need.items():
                            if waited.get(k, 0) >= v:
                                continue
                            e.wait_ge(sems[k], v)
                            waited[k] = v
                        if op.fn is not None:
                            ins = op.fn(e)
                            if op.is_dma:
                                ins.then_inc(sems[op.sem], 16)
                            elif op.signal:
                                ins.then_inc(sems[op.sem], 1)
                return body

            block.tensor(make("pe"))
            block.scalar(make("act"))
            block.vector(make("dve"))
            block.gpsimd(make("pool"))
            block.sync(make("sp"))


class Arena:
    def __init__(self, base_ap, start=0):
        self.base = base_ap
        self.off = start

    def fork(self):
        return Arena(self.base, self.off)

    def get(self, dims, dt):
        n = int(np.prod(dims))
        if dt == F32:
            n *= 2
        a = self.base[:, self.off:self.off + n]
        self.off += (n + 15) // 16 * 16
        assert self.off <= self.base.shape[1], ("arena overflow", self.off)
        if dt == F32:
            a = a.bitcast(F32)
        if len(dims) == 2:
            a = a.rearrange("p (a b) -> p a b", a=dims[0])
        elif len(dims) == 3:
            a = a.rearrange("p (a b c) -> p a b c", a=dims[0], b=dims[1])
        elif len(dims) == 4:
            a = a.rearrange("p (a b c d) -> p a b c d", a=dims[0], b=dims[1], c=dims[2])
        return a


def bc(ap, shape):
    return ap.broadcast_to(list(shape))


def build_program(debug=False):
    nc = bass.Bass("TRN2", target_bir_lowering=False)

    def din(name, shape):
        return nc.dram_tensor(name, list(shape), F32, kind="ExternalInput").ap()

    x_own = din("x_own", [TOK, D])
    x_oth = din("x_oth", [TOK, D])
    w_in = din("w_in", [D, 4096])
    w_o_attn = din("w_o_attn", [512, D])
    w_o_pool = din("w_o_pool", [512, D])
    w_out = din("w_out", [D, D])
    w_ff1 = din("w_ff1", [D, DFF])
    w_ff2 = din("w_ff2", [DFF, D])
    w_pool = din("w_pool", [4, 128, 128])
    rel_table = din("rel_table", [32, 8])
    lnvec = din("lnvec", [5, D])
    bgate_l = din("bgate_l", [128, 16])
    bff1_l = din("bff1_l", [128, 32])
    pscale_l = din("pscale_l", [128, 4])
    ident_f = din("ident_f", [128, 128])
    jmat_f = din("jmat_f", [128, 128])
    oht_f = din("oht_f", [33, 384])
    gbias_f = din("gbias_f", [128, 512])
    rc0_f = din("rc0_f", [128, 64])
    out = nc.dram_tensor("out", [TOK, D], F32, kind="ExternalOutput").ap()
    bv_scr = nc.dram_tensor("bv_scr", [8, 384], F32, kind="Internal").ap()
    wff1_b = nc.dram_tensor("wff1_b", [128, 8 * DFF], BF16, kind="Internal").ap().rearrange("p (k n) -> p k n", k=8)
    wff2_b = nc.dram_tensor("wff2_b", [128, 32 * D], BF16, kind="Internal").ap().rearrange("p (f n) -> p f n", f=32)

    arena_t = nc.alloc_sbuf_tensor("arena", [128, 106000], BF16)
    AR = Arena(arena_t[:, :])
    ps = [nc.alloc_psum_tensor("ps%d" % b, [128, 512], F32) for b in range(8)]
    psb = [p.bitcast(BF16) for p in ps]

    P = Prog()
    uid = [0]

    def dkey(s):
        uid[0] += 1
        return "%s_%d" % (s, uid[0])

    ident = AR.get([128], BF16)
    aT = AR.get([4, TOK], BF16)

    w_in_v = w_in.rearrange("(kc p) n -> p kc n", p=128)

    A1 = AR.fork()
    jb = A1.get([128], BF16)
    E01 = A1.get([8, 256], BF16)
    cfar = A1.get([8], F32)
    gbias = A1.get([16, 32], F32)
    vm = A1.get([16, 32], F32)
    S0_START = 81000
    S0 = Arena(AR.base, S0_START)
    tabf = S0.get([8], F32)
    tab_hi = S0.get([8], BF16)
    tab_lo = S0.get([8], BF16)
    oht = S0.get([384], BF16)
    bvs = S0.get([384], F32)
    hk = S0.get([16, 128], F32)
    hk_hi = S0.get([16, 128], BF16)
    hk_lo = S0.get([16, 128], BF16)

    P.add("pool", lambda e: e.dma_start(out=ident, in_=ident_f), writes=["ident"], dma=dkey("c"))
    P.add("pool", lambda e: e.dma_start(out=jb, in_=jmat_f), writes=["jb"], dma=dkey("c"))
    P.add("pool", lambda e: e.dma_start(out=oht[0:33], in_=oht_f), writes=["oht"], dma=dkey("c"))
    P.add("sp", lambda e: e.dma_start(out=tabf[0:32], in_=rel_table), writes=["tabf_a"], dma=dkey("c"))
    P.add("dve", lambda e: e.memset(tabf[32:33], -30000.0), writes=["tabf_b"])
    P.add("sp", lambda e: e.dma_start(out=cfar, in_=bass.AP(rel_table.tensor, 31 * 8, [[0, 128], [1, 8]])),
          writes=["cfar"], dma=dkey("c"))
    P.add("sp", lambda e: e.dma_start(out=gbias.rearrange("p a b -> p (a b)"), in_=gbias_f),
          writes=["gbias"], dma=dkey("c"))
    P.add("dve", lambda e: e.tensor_single_scalar(out=vm, in_=gbias, scalar=-1.0, op=ALU.is_ge),
          reads=["gbias"], writes=["vm"])
    P.add("dve", lambda e: e.tensor_copy(out=tab_hi[0:33], in_=tabf[0:33]), reads=["tabf_a", "tabf_b"], writes=["tab_hi"])
    P.add("dve", lambda e: e.tensor_tensor(out=tab_lo[0:33], in0=tabf[0:33], in1=tab_hi[0:33], op=ALU.subtract),
          reads=["tabf_a", "tabf_b", "tab_hi"], writes=["tab_lo"])
    P.add("pe", lambda e: e.matmul(ps[0][0:8, 0:384], lhsT=tab_hi[0:33], rhs=oht[0:33], start=True, stop=False),
          reads=["tab_hi", "oht"], writes=["ps0"])
    P.add("pe", lambda e: e.matmul(ps[0][0:8, 0:384], lhsT=tab_lo[0:33], rhs=oht[0:33], start=False, stop=True),
          reads=["tab_lo", "oht"], writes=["ps0"])
    P.add("act", lambda e: e.activation(out=bvs[0:8], in_=ps[0][0:8, 0:384], func=AF.Copy), reads=["ps0"], writes=["bvs"])
    P.add("sp", lambda e: e.dma_start(out=bv_scr, in_=bvs[0:8]), reads=["bvs"], writes=["bv_scr"], dma=dkey("c"))
    for g in range(16):
        h, oi = g // 2, g % 2
        P.add("sp", (lambda g, h, oi: lambda e: e.dma_start(
            out=hk[:, g, :], in_=bass.AP(bv_scr.tensor, h * 384 + oi * 128, [[1, 128], [1, 128]])))(g, h, oi),
            reads=["bv_scr"], writes=["hk%d" % g], dma=dkey("hk"))
    hkr = ["hk%d" % g for g in range(16)]
    def setup_late():
        P.add("dve", lambda e: e.tensor_copy(out=hk_hi, in_=hk), reads=hkr, writes=["hk_hi"])
        P.add("dve", lambda e: e.tensor_tensor(out=hk_lo, in0=hk, in1=hk_hi, op=ALU.subtract), reads=hkr + ["hk_hi"], writes=["hk_lo"])
        for g in range(16):
            b, q = 1 + g // 4, g % 4
            P.add("pe", (lambda g, b, q: lambda e: e.matmul(ps[b][:, q * 128:(q + 1) * 128], lhsT=jb, rhs=hk_hi[:, g, :],
                                                           start=True, stop=False))(g, b, q),
                  reads=["jb", "hk_hi"], writes=["ps%d" % b])
            P.add("pe", (lambda g, b, q: lambda e: e.matmul(ps[b][:, q * 128:(q + 1) * 128], lhsT=jb, rhs=hk_lo[:, g, :],
                                                           start=False, stop=True))(g, b, q),
                  reads=["jb", "hk_lo"], writes=["ps%d" % b])
        for b4 in range(4):
            P.add("act", (lambda b4: lambda e: e.activation(
                out=E01[:, 2 * b4:2 * b4 + 2, :].rearrange("p a b -> p (a b)"), in_=ps[1 + b4][:, :], func=AF.Exp))(b4),
                reads=["ps%d" % (1 + b4)], writes=["E01"])
    SETUP_SCR = ["tabf_a", "tabf_b", "tab_hi", "tab_lo", "oht", "bvs", "hk_hi", "hk_lo"] + hkr

    A2 = A1.fork()
    wqkv = A2.get([8, 768], BF16)
    kto = A2.get([2, 16, 256], BF16)
    kown = A2.get([2, 16, 256], BF16)
    vo = A2.get([16, 2, 4, 65], BF16)
    vown = A2.get([16, 2, 4, 65], BF16)
    ksum = A2.get([2, 32], F32)
    kmean = A2.get([2, 32], BF16)
    xs = [A2.get([2, D], BF16) for _ in range(2)]
    xT = A2.get([8, 256], BF16)
    qT = A2.get([4, 256], BF16)
    gsb = A2.get([8, 32], F32)
    top8 = A2.get([8, 8], F32)
    msk = A2.get([8, 32], F32)
    PT = [A2.get([2, 256], BF16) for _ in range(4)]
    Oacc = A2.get([2, 4, 65], F32)
    rec = A2.get([2, 4], F32)
    atok = A2.get([2, 256], BF16)
    stg = [A2.get([8, 512], BF16) for _ in range(2)]
    PH2A_END = A2.off
    TAIL = Arena(AR.base, PH2A_END)
    wgp = TAIL.get([8, 2560], BF16)
    woa = TAIL.get([4, D], BF16)
    assert PH2A_END <= S0_START
    w_ff1_v = w_ff1.rearrange("(kc p) n -> p kc n", p=128)
    w_ff2_v = w_ff2.rearrange("(f p) n -> p f n", p=128)
    stg_ctr = [0]

    def stage_round():
        r = stg_ctr[0]
        if r >= 16:
            return
        stg_ctr[0] += 1
        sl = r % 2
        if r < 8:
            src = w_ff1_v[:, :, r * 512:(r + 1) * 512]
            dst = wff1_b[:, :, r * 512:(r + 1) * 512]
            sv = stg[sl]
            name = "wsc1_%d" % r
        else:
            src = w_ff2_v[:, (r - 8) * 4:(r - 7) * 4, :]
            dst = wff2_b[:, (r - 8) * 4:(r - 7) * 4, :]
            sv = stg[sl].rearrange("p a b -> p (a b)").rearrange("p (f n) -> p f n", f=4)
            name = "wsc2_%d" % (r - 8)
        P.add("pool", (lambda sv, src: lambda e: e.dma_start(out=sv, in_=src))(sv, src), writes=["stg%d" % sl], dma="stg%d" % sl)
        P.add("sp", (lambda sv, dst: lambda e: e.dma_start(out=dst, in_=sv))(sv, dst), reads=["stg%d" % sl], writes=[name],
              dma="stgst%d" % sl)

    first_wq = [True]
    xs_ctr = [0]
    pt_ctr = [0]
    s_ctr = [0]
    o_ctr = [0]
    SBANKS = [3, 5, 6]
    OBANKS = [4, 7]
    NPT = 4
    LAG = 2

    def load_block_xT(src, blk, banks=(0, 0), act_evac=True):
        slot = xs_ctr[0] % 2
        xs_ctr[0] += 1
        srcv = src[blk * 256:(blk + 1) * 256, :].rearrange("(t p) d -> p t d", p=128)
        P.add("pool", (lambda slot, srcv: lambda e: e.dma_start(out=xs[slot], in_=srcv))(slot, srcv),
              writes=["xs%d" % slot], dma="xs%d" % slot)
        for tt in range(2):
            tb = banks[tt]
            for kc in range(8):
                P.add("pe", (lambda slot, tt, kc, tb: lambda e: e.transpose(
                    out=psb[tb][:, kc * 128:(kc + 1) * 128], in_=xs[slot][:, tt, kc * 128:(kc + 1) * 128],
                    identity=ident))(slot, tt, kc, tb), reads=["xs%d" % slot, "ident"], writes=["ps%d" % tb])
            src_v = psb[tb][:, :].rearrange("p (k c) -> p k c", k=8)
            if tt == 0 or not act_evac:
                P.add("dve", (lambda tt, src_v: lambda e: e.tensor_copy(out=xT[:, :, tt * 128:(tt + 1) * 128], in_=src_v))(tt, src_v),
                      reads=["ps%d" % tb], writes=["xT%d" % tt])
            else:
                P.add("act", (lambda tt, src_v: lambda e: e.activation(out=xT[:, :, tt * 128:(tt + 1) * 128], in_=src_v, func=AF.Copy))(tt, src_v),
                      reads=["ps%d" % tb], writes=["xT%d" % tt])

    def proj_kv(kdst, vdst, j, kcol):
        for pl in range(2):
            for kc in range(8):
                P.add("pe", (lambda pl, kc: lambda e: e.matmul(
                    ps[1][:, pl * 256:(pl + 1) * 256], lhsT=wqkv[:, kc, 256 + pl * 128:256 + (pl + 1) * 128],
                    rhs=xT[:, kc, :], start=(kc == 0), stop=(kc == 7)))(pl, kc),
                    reads=["wqkv", "xT0", "xT1"], writes=["ps1"])
        for pl in range(2):
            P.add("act", (lambda pl: lambda e: e.activation(
                out=kdst[:, pl, j, :], in_=ps[1][:, pl * 256:(pl + 1) * 256], func=AF.Copy,
                accum_out=ksum[:, pl, kcol:kcol + 1]))(pl),
                reads=["ps1"], writes=["%s%d" % ("k", id(kdst)) + "_%d" % j, "ksum"])
        for tt in range(2):
            for kc in range(8):
                P.add("pe", (lambda tt, kc: lambda e: e.matmul(
                    ps[1][:, tt * 256:(tt + 1) * 256], lhsT=xT[:, kc, tt * 128:(tt + 1) * 128],
                    rhs=wqkv[:, kc, 512:768], start=(kc == 0), stop=(kc == 7)))(tt, kc),
                    reads=["wqkv", "xT0", "xT1"], writes=["ps1"])
        P.add("dve", lambda e: e.tensor_copy(
            out=vdst[:, j, :, :, 0:64], in_=ps[1][:, :].rearrange("p (t h d) -> p t h d", t=2, h=4)),
            reads=["ps1"], writes=["v%d_%d" % (id(vdst), j), "vones"])

    def kname(kdst, j):
        return "k%d_%d" % (id(kdst), j)

    def vname(vdst, j):
        return "v%d_%d" % (id(vdst), j)

    for hp in range(2):
        for part, c0 in enumerate((hp * 256, 512 + hp * 256, 1024 + hp * 256)):
            rd = []
            P.add("pool", (lambda part, c0: lambda e: e.dma_start(
                out=wqkv[:, :, part * 256:(part + 1) * 256], in_=w_in_v[:, :, c0:c0 + 256]))(part, c0),
                reads=[], writes=["wqkv"] + rd, dma=dkey("wqkv"))
            first_wq[0] = False
        P.add("dve", lambda e: e.memset(vo.rearrange("p a b c d -> p (a b c d)"), 1.0),
              writes=["vones"] + [vname(vo, j) for j in range(16)])
        P.add("dve", lambda e: e.memset(vown.rearrange("p a b c d -> p (a b c d)"), 1.0),
              writes=["vones"] + [vname(vown, j) for j in range(16)])
        P.add("dve", lambda e: e.memset(ksum, 0.0), writes=["ksum"])
        P.add("dve", lambda e: e.memset(kmean, 0.0), writes=["kmean"])
        P.add("dve", lambda e: e.memset(qT.rearrange("p a b -> p (a b)"), 0.0), writes=["qT"])

        for j in range(NBLK):
            load_block_xT(x_oth, j, banks=(0, 2))
            proj_kv(kto, vo, j, 16 + j)
        P.add("dve", lambda e: e.tensor_scalar_mul(out=kmean[:, :, 16:32], in0=ksum[:, :, 16:32], scalar1=1.0 / 256.0),
              reads=["ksum"], writes=["kmean"])

        if hp == 0:
            setup_late()
        if hp == 1:
            for q4 in range(4):
                P.add("pool", (lambda q4: lambda e: e.dma_start(
                    out=wgp[:, :, q4 * 640:(q4 + 1) * 640], in_=w_in_v[:, :, 1536 + q4 * 640:1536 + (q4 + 1) * 640]))(q4),
                    writes=["wgp%d" % q4] + SETUP_SCR, dma=dkey("wgp"))
            P.add("pool", lambda e: e.dma_start(out=woa, in_=w_o_attn.rearrange("(c p) n -> p c n", p=128)),
                  writes=["woa"] + SETUP_SCR, dma=dkey("woa"))
        for i in range(NBLK):
            load_block_xT(x_own, i, act_evac=False)
            stage_round()
            for pl in range(2):
                for kc in range(8):
                    P.add("pe", (lambda pl, kc: lambda e: e.matmul(
                        ps[2][:, pl * 256:(pl + 1) * 256], lhsT=wqkv[:, kc, pl * 128:(pl + 1) * 128],
                        rhs=xT[:, kc, :], start=(kc == 0), stop=(kc == 7)))(pl, kc),
                        reads=["wqkv", "xT0", "xT1"], writes=["ps2"])
            for hl in range(4):
                pl, eh = hl // 2, hl % 2
                P.add("dve", (lambda hl, pl, eh: lambda e: e.tensor_copy(
                    out=qT[eh * 64:(eh + 1) * 64, hl, :], in_=ps[2][eh * 64:(eh + 1) * 64, pl * 256:(pl + 1) * 256]))(hl, pl, eh),
                    reads=["ps2"], writes=["qT"])
            proj_kv(kown, vown, i, i)
            for hl in range(4):
                pl, eh = hl // 2, hl % 2
                for qt in range(2):
                    gidx = hl * 2 + qt
                    P.add("pe", (lambda pl, hl, qt, gidx: lambda e: e.matmul(
                        ps[0][:, gidx * 32:(gidx + 1) * 32],
                        lhsT=qT[:, hl, qt * 128:(qt + 1) * 128],
                        rhs=kmean[:, pl, :], start=True, stop=True))(pl, hl, qt, gidx),
                        reads=["qT", "kmean"], writes=["ps0"])
            P.add("dve", (lambda i: lambda e: e.tensor_tensor(
                out=gsb, in0=ps[0][:, 0:256].rearrange("p (a b) -> p a b", a=8),
                in1=bc(gbias[:, i:i + 1, :], [128, 8, 32]), op=ALU.add))(i),
                reads=["ps0", "gbias"], writes=["gsb"])
            for g in range(8):
                P.add("dve", (lambda g: lambda e: e.max(out=top8[:, g, :], in_=gsb[:, g, :]))(g),
                      reads=["gsb"], writes=["top8_%d" % g])
            P.add("dve", lambda e: e.tensor_tensor(out=msk, in0=gsb, in1=bc(top8[:, :, 2:3], [128, 8, 32]), op=ALU.is_ge),
                  reads=["gsb"] + ["top8_%d" % g for g in range(8)], writes=["msk"])
            P.add("dve", (lambda i: lambda e: e.tensor_tensor(out=msk, in0=msk, in1=bc(vm[:, i:i + 1, :], [128, 8, 32]),
                                                             op=ALU.mult))(i),
                  reads=["msk", "vm"], writes=["msk"])
            P.add("dve", (lambda i: lambda e: e.tensor_scalar_mul(out=kmean[:, :, i:i + 1], in0=ksum[:, :, i:i + 1],
                                                                 scalar1=1.0 / 256.0))(i),
                  reads=["ksum"], writes=["kmean"])

            visits = [("self", kown, vown, i, None)]
            visits += [("far", kown, vown, j, j) for j in range(i)]
            visits += [("far", kto, vo, j, 16 + j) for j in range(i)]
            visits += [("prev", kto, vo, i, 16 + i)]
            hvs = [(kind, kb, vb, j, col, hl) for (kind, kb, vb, j, col) in visits for hl in range(4)]

            def qk_exp(n):
                kind, kb, vb, j, col, hl = hvs[n]
                pl = hl // 2
                h = hp * 4 + hl
                sb = SBANKS[s_ctr[0] % len(SBANKS)]
                s_ctr[0] += 1
                pslot = pt_ctr[0] % NPT
                pt_ctr[0] += 1
                ptn = "PT%d" % pslot
                for kt in range(2):
                    P.add("pe", (lambda sb, kb, pl, hl, j, kt: lambda e: e.matmul(
                        ps[sb][:, kt * 256:(kt + 1) * 256],
                        lhsT=kb[:, pl, j, kt * 128:(kt + 1) * 128],
                        rhs=qT[:, hl, :], start=True, stop=True))(sb, kb, pl, hl, j, kt),
                        reads=[kname(kb, j), "qT"], writes=["ps%d" % sb])
                ptf = PT[pslot].rearrange("p a b -> p (a b)")
                if kind == "self":
                    P.add("act", (lambda sb, pslot: lambda e: e.activation(
                        out=PT[pslot][:, 0, :], in_=ps[sb][:, 0:256], func=AF.Exp, scale=0.125))(sb, pslot),
                        reads=["ps%d" % sb], writes=[ptn])
                    P.add("act", (lambda sb, pslot: lambda e: e.activation(
                        out=PT[pslot][:, 1, 128:256], in_=ps[sb][:, 384:512], func=AF.Exp, scale=0.125))(sb, pslot),
                        reads=["ps%d" % sb], writes=[ptn])
                    P.add("dve", (lambda pslot, h: lambda e: e.tensor_tensor(
                        out=PT[pslot][:, 0, :], in0=PT[pslot][:, 0, :], in1=E01[:, h, :], op=ALU.mult))(pslot, h),
                        reads=[ptn, "E01"], writes=[ptn])
                    P.add("dve", (lambda pslot, h: lambda e: e.tensor_tensor(
                        out=PT[pslot][:, 1, 128:256], in0=PT[pslot][:, 1, 128:256], in1=E01[:, h, 0:128],
                        op=ALU.mult))(pslot, h),
                        reads=[ptn, "E01"], writes=[ptn])
                elif kind == "far":
                    P.add("act", (lambda sb, ptf, h: lambda e: e.activation(
                        out=ptf, in_=ps[sb][:, :], func=AF.Exp, scale=0.125, bias=cfar[:, h:h + 1]))(sb, ptf, h),
                        reads=["ps%d" % sb, "cfar"], writes=[ptn])
                else:
                    P.add("act", (lambda sb, pslot, h: lambda e: e.activation(
                        out=PT[pslot][:, 0, :], in_=ps[sb][:, 0:256], func=AF.Exp, scale=0.125,
                        bias=cfar[:, h:h + 1]))(sb, pslot, h),
                        reads=["ps%d" % sb, "cfar"], writes=[ptn])
                    P.add("act", (lambda sb, pslot, h: lambda e: e.activation(
                        out=PT[pslot][:, 1, 128:256], in_=ps[sb][:, 384:512], func=AF.Exp, scale=0.125,
                        bias=cfar[:, h:h + 1]))(sb, pslot, h),
                        reads=["ps%d" % sb, "cfar"], writes=[ptn])
                    if kind == "prev":
                        P.add("act", (lambda sb, pslot: lambda e: e.activation(
                            out=PT[pslot][:, 1, 0:128], in_=ps[sb][:, 256:384], func=AF.Exp, scale=0.125))(sb, pslot),
                            reads=["ps%d" % sb], writes=[ptn])
                        P.add("dve", (lambda pslot, h: lambda e: e.tensor_tensor(
                            out=PT[pslot][:, 1, 0:128], in0=PT[pslot][:, 1, 0:128], in1=E01[:, h, 128:256],
                            op=ALU.mult))(pslot, h),
                            reads=[ptn, "E01"], writes=[ptn])
                return pslot

            def pv_acc(n, pslot):
                kind, kb, vb, j, col, hl = hvs[n]
                ptn = "PT%d" % pslot
                ob = OBANKS[o_ctr[0] % len(OBANKS)]
                o_ctr[0] += 1
                on = "ps%d" % ob
                for qt in range(2):
                    kts = [0] if (kind == "self" and qt == 0) else [0, 1]
                    for kt in kts:
                        P.add("pe", (lambda ob, qt, kt, pslot, vb, j, hl, kts: lambda e: e.matmul(
                            ps[ob][:, qt * 65:(qt + 1) * 65],
                            lhsT=PT[pslot][:, kt, qt * 128:(qt + 1) * 128],
                            rhs=vb[:, j, kt, hl, :], start=(kt == kts[0]), stop=(kt == kts[-1])))(
                                ob, qt, kt, pslot, vb, j, hl, kts),
                            reads=[ptn, vname(vb, j), "vones"], writes=[on])
                if kind == "self":
                    P.add("dve", (lambda ob, hl: lambda e: e.tensor_copy(
                        out=Oacc[:, :, hl, :], in_=ps[ob][:, 0:130].rearrange("p (a b) -> p a b", a=2)))(ob, hl),
                        reads=[on], writes=["Oacc%d" % hl])
                else:
                    for qt in range(2):
                        P.add("dve", (lambda ob, hl, qt, col: lambda e: e.scalar_tensor_tensor(
                            out=Oacc[:, qt, hl, :], in0=ps[ob][:, qt * 65:(qt + 1) * 65],
                            scalar=msk[:, hl * 2 + qt, col:col + 1], in1=Oacc[:, qt, hl, :],
                            op0=ALU.mult, op1=ALU.add))(ob, hl, qt, col),
                            reads=[on, "msk", "Oacc%d" % hl], writes=["Oacc%d" % hl])

            slots = {}
            for n in range(len(hvs)):
                slots[n] = qk_exp(n)
                if n >= LAG:
                    pv_acc(n - LAG, slots.pop(n - LAG))
            for n in range(max(0, len(hvs) - LAG), len(hvs)):
                pv_acc(n, slots.pop(n))
            oar = ["Oacc%d" % hl for hl in range(4)]
            P.add("dve", lambda e: e.reciprocal(out=rec, in_=Oacc[:, :, :, 64]), reads=oar, writes=["rec"])
            P.add("dve", lambda e: e.tensor_tensor(
                out=atok.rearrange("p a (h d) -> p a h d", h=4), in0=Oacc[:, :, :, 0:64],
                in1=bc(rec.unsqueeze(3), [128, 2, 4, 64]), op=ALU.mult),
                reads=oar + ["rec"], writes=["atok"])
            for c in range(2):
                for qt in range(2):
                    P.add("pe", (lambda c, qt: lambda e: e.transpose(
                        out=psb[0][:, (c * 2 + qt) * 128:(c * 2 + qt + 1) * 128],
                        in_=atok[:, qt, c * 128:(c + 1) * 128], identity=ident))(c, qt),
                        reads=["atok", "ident"], writes=["ps0"])
            P.add("dve", (lambda hp, i: lambda e: e.tensor_copy(
                out=aT[:, hp * 2:hp * 2 + 2, i * 256:(i + 1) * 256],
                in_=psb[0][:, 0:512].rearrange("p (c q) -> p c q", c=2)))(hp, i),
                reads=["ps0"], writes=["aT%d" % i])

    PH2A = ["wqkv", "ksum", "kmean", "xs0", "xs1", "xT0", "xT1", "qT", "gsb", "msk", "PT0", "PT1", "PT2", "PT3",
            "Oacc0", "Oacc1", "Oacc2", "Oacc3", "rec", "atok", "stg0", "stg1", "E01", "cfar", "gbias", "vm", "vones"] + \
           ["top8_%d" % g for g in range(8)] + \
           [kname(kto, j) for j in range(16)] + [kname(kown, j) for j in range(16)] + \
           [vname(vo, j) for j in range(16)] + [vname(vown, j) for j in range(16)]

    B = AR.fork()
    wop = B.get([4, D], BF16)
    wout = B.get([8, D], BF16)
    wpl = B.get([4, 128], BF16)
    lnc = B.get([5, D], F32)
    bgate = B.get([16], F32)
    bff1 = B.get([32], F32)
    pscale = B.get([4], F32)
    rc0 = B.get([4, 16], F32)
    xs2 = B.get([2, D], BF16)
    xh = B.get([D], BF16)
    xT2 = B.get([8, 272], BF16)
    pf = B.get([4, 272], F32)
    wA = B.get([4, 272], F32)
    wB = B.get([4, 272], F32)
    pooled = B.get([4, 256], BF16)
    ptmp = B.get([16], F32)
    BT = B.get([4, 256], BF16)
    sg = [B.get([2, 256], F32) for _ in range(2)]
    t12 = sg
    uT = B.get([8, 256], BF16)
    xf = B.get([2, D], F32)
    hb = xs2
    hT = B.get([8, 256], BF16)
    stats = B.get([2, 2, 6], F32)
    mv = B.get([2, 2], F32)
    rstd = B.get([2], F32)
    rr = [B.get([256], F32) for _ in range(2)]
    NWS = 3
    rT = [B.get([FG, 256], BF16) for _ in range(NWS)]
    w1c = [B.get([8, FG * 128], BF16) for _ in range(NWS)]
    w2c = [B.get([FG, D], BF16) for _ in range(NWS)]

    assert B.off <= PH2A_END and TAIL.off <= 106000
    first2b = [True]

    def wdma(dst, src, name, eng="pool"):
        extra = (PH2A + ["bar2b"]) if first2b[0] else []
        rd = [] if first2b[0] else ["bar2b"]
        first2b[0] = False
        P.add(eng, (lambda dst, src: lambda e: e.dma_start(out=dst, in_=src))(dst, src),
              reads=rd, writes=[name] + extra, dma=dkey(name))

    wdma(bgate, bgate_l, "bgate", eng="sp")
    WGP = ["wgp%d" % q for q in range(4)]
    wdma(wop, w_o_pool.rearrange("(c p) n -> p c n", p=128), "wop")
    wdma(wout, w_out.rearrange("(c p) n -> p c n", p=128), "wout")
    wdma(wpl, w_pool.rearrange("g c d -> c g d"), "wpl")
    for k in range(5):
        P.add("sp", (lambda k: lambda e: e.dma_start(out=lnc[:, k, :], in_=bass.AP(lnvec.tensor, k * D, [[0, 128], [1, D]])))(k),
              reads=["bar2b"], writes=["lnc%d" % k], dma=dkey("lnc"))
    wdma(bff1, bff1_l, "bff1", eng="sp")
    wdma(pscale, pscale_l, "pscale", eng="sp")
    wdma(rc0.rearrange("p a b -> p (a b)"), rc0_f, "rc0", eng="sp")

    def layer_norm(gk, bk):
        for tt in range(2):
            for hf in range(2):
                P.add("dve", (lambda tt, hf: lambda e: e.bn_stats(out=stats[:, tt, hf, :], in_=xf[:, tt, hf * 512:(hf + 1) * 512]))(tt, hf),
                      reads=["xf%d" % tt], writes=["stats%d%d" % (tt, hf)])
            P.add("dve", (lambda tt: lambda e: e.bn_aggr(out=mv[:, tt, :], in_=stats[:, tt, :, :]))(tt),
                  reads=["stats%d0" % tt, "stats%d1" % tt], writes=["mv%d" % tt])
        P.add("dve", lambda e: e.tensor_scalar_add(out=rstd, in0=mv[:, :, 1], scalar1=EPS), reads=["mv0", "mv1"], writes=["rstd"])
        P.add("act", lambda e: e.activation(out=rstd, in_=rstd, func=AF.Sqrt), reads=["rstd"], writes=["rstd"])
        P.add("dve", lambda e: e.reciprocal(out=rstd, in_=rstd), reads=["rstd"], writes=["rstd"])
        for tt in range(2):
            P.add("dve", (lambda tt: lambda e: e.scalar_tensor_tensor(
                out=xf[:, tt, :], in0=xf[:, tt, :], scalar=mv[:, tt, 0:1], in1=lnc[:, gk, :],
                op0=ALU.subtract, op1=ALU.mult))(tt),
                reads=["xf%d" % tt, "mv%d" % tt, "lnc%d" % gk], writes=["xf%d" % tt])
            P.add("dve", (lambda tt: lambda e: e.scalar_tensor_tensor(
                out=xf[:, tt, :], in0=xf[:, tt, :], scalar=rstd[:, tt:tt + 1], in1=lnc[:, bk, :],
                op0=ALU.mult, op1=ALU.add))(tt),
                reads=["xf%d" % tt, "rstd", "lnc%d" % bk], writes=["xf%d" % tt])

    wslot = [0]
    fbank = [0]
    XT2 = ["xT2h", "xT2_0", "xT2_1"]
    PF = ["pf%d" % g for g in range(4)]

    def own_view(i):
        return x_own[i * 256:(i + 1) * 256, :].rearrange("(t p) d -> p t d", p=128)

    def X_load(i):
        P.add("pool", (lambda v: lambda e: e.dma_start(out=xs2, in_=v))(own_view(i)), reads=["bar2b"], writes=["xs2"], dma="xs2")
        P.add("pool", (lambda i: lambda e: e.dma_start(out=xh[0:16], in_=x_oth[i * 256 + 240:(i + 1) * 256, :]))(i),
              reads=["bar2b"], writes=["xh"], dma="xh")

    def XF_load(i):
        P.add("sp", (lambda v: lambda e: e.dma_start(out=xf, in_=v))(own_view(i)), reads=["bar2b"], writes=["xf0", "xf1"], dma="xf")

    def X(i):
        for kc in range(8):
            P.add("pe", (lambda kc: lambda e: e.transpose(out=psb[0][:, kc * 16:(kc + 1) * 16], in_=xh[0:16, kc * 128:(kc + 1) * 128],
                                                         identity=ident[0:16, 0:16]))(kc),
                  reads=["xh", "ident"], writes=["ps0"])
        P.add("act", lambda e: e.activation(out=xT2[:, :, 0:16], in_=psb[0][:, 0:128].rearrange("p (k c) -> p k c", k=8), func=AF.Copy),
              reads=["ps0"], writes=["xT2h"])
        for tt in range(2):
            for kc in range(8):
                P.add("pe", (lambda tt, kc: lambda e: e.transpose(
                    out=psb[0][:, kc * 128:(kc + 1) * 128], in_=xs2[:, tt, kc * 128:(kc + 1) * 128], identity=ident))(tt, kc),
                    reads=["xs2", "ident"], writes=["ps0"])
            P.add("act", (lambda tt: (lambda e: e.activation(
                out=xT2[:, :, 16 + tt * 128:16 + (tt + 1) * 128], in_=psb[0][:, :].rearrange("p (k c) -> p k c", k=8), func=AF.Copy))
                if True else None)(tt),
                reads=["ps0"], writes=["xT2_%d" % tt])

    def Pm_a(i):
        for g in range(4):
            pb = 1 + (g % 2)
            for kc in range(8):
                P.add("pe", (lambda g, pb, kc: lambda e: e.matmul(
                    ps[pb][:, 0:272], lhsT=wgp[:, kc, g * 128:(g + 1) * 128], rhs=xT2[:, kc, :],
                    start=(kc == 0), stop=(kc == 7)))(g, pb, kc),
                    reads=XT2 + WGP, writes=["ps%d" % pb])
            P.add("act", (lambda g, pb: lambda e: e.activation(out=pf[:, g, :], in_=ps[pb][:, 0:272], func=AF.Copy))(g, pb),
                  reads=["ps%d" % pb], writes=["pf%d" % g])

    def Pm_b(i):
        P.add("dve", lambda e: e.tensor_tensor(out=wA[:, :, 1:272], in0=pf[:, :, 1:272], in1=pf[:, :, 0:271], op=ALU.add),
              reads=PF, writes=["wA"])
        P.add("dve", lambda e: e.tensor_tensor(out=wB[:, 1:4, 3:272], in0=wA[:, 1:4, 3:272], in1=wA[:, 1:4, 1:270], op=ALU.add),
              reads=["wA"], writes=["wB"])
        P.add("dve", lambda e: e.tensor_tensor(out=wA[:, 2:4, 7:272], in0=wB[:, 2:4, 7:272], in1=wB[:, 2:4, 3:268], op=ALU.add),
              reads=["wB"], writes=["wA"])
        P.add("dve", lambda e: e.tensor_tensor(out=wB[:, 3, 15:272], in0=wA[:, 3, 15:272], in1=wA[:, 3, 7:264], op=ALU.add),
              reads=["wA"], writes=["wB"])
        for g in range(4):
            fin_ = wA if g % 2 == 0 else wB
            P.add("dve", (lambda g, fin_: lambda e: e.scalar_tensor_tensor(
                out=pooled[:, g, :], in0=fin_[:, g, 16:272], scalar=1.0 / WINS[g], in1=pf[:, g, 16:272],
                op0=ALU.mult, op1=ALU.subtract))(g, fin_),
                reads=["wA", "wB"] + PF, writes=["pooled%d" % g])
            if i == 0:
                P.add("dve", (lambda g, fin_: lambda e: e.tensor_tensor(out=ptmp, in0=fin_[:, g, 16:32], in1=rc0[:, g, :], op=ALU.mult))(g, fin_),
                      reads=["wA", "wB", "rc0"], writes=["ptmp"])
                P.add("dve", (lambda g: lambda e: e.tensor_tensor(out=pooled[:, g, 0:16], in0=ptmp, in1=pf[:, g, 16:32], op=ALU.subtract))(g),
                      reads=["ptmp"] + PF, writes=["pooled%d" % g])
        for g in range(4):
            pb = 1 + (g // 2)
            P.add("pe", (lambda g, pb: lambda e: e.matmul(
                ps[pb][:, (g % 2) * 256:(g % 2 + 1) * 256], lhsT=wpl[:, g, :], rhs=pooled[:, g, :], start=True, stop=True))(g, pb),
                reads=["wpl", "pooled%d" % g], writes=["ps%d" % pb])
        for g in range(4):
            pb = 1 + (g // 2)
            P.add("dve", (lambda g, pb: lambda e: e.tensor_scalar_mul(
                out=BT[:, g, :], in0=ps[pb][:, (g % 2) * 256:(g % 2 + 1) * 256], scalar1=pscale[:, g:g + 1]))(g, pb),
                reads=["ps%d" % pb, "pscale"], writes=["BT"])

    def G(i):
        for dc in range(8):
            s = dc % 2
            ba, bb = 1 + 2 * s, 2 + 2 * s
            for kc in range(8):
                P.add("pe", (lambda ba, dc, kc: lambda e: e.matmul(
                    ps[ba][:, 0:256], lhsT=wgp[:, kc, 512 + dc * 128:512 + (dc + 1) * 128], rhs=xT2[:, kc, 16:272],
                    start=(kc == 0), stop=(kc == 7)))(ba, dc, kc), reads=XT2 + WGP, writes=["ps%d" % ba])
            for kc in range(8):
                P.add("pe", (lambda ba, dc, kc: lambda e: e.matmul(
                    ps[ba][:, 256:512], lhsT=wgp[:, kc, 1536 + dc * 128:1536 + (dc + 1) * 128], rhs=xT2[:, kc, 16:272],
                    start=(kc == 0), stop=(kc == 7)))(ba, dc, kc), reads=XT2 + WGP, writes=["ps%d" % ba])
            for c in range(4):
                P.add("pe", (lambda bb, dc, c, i: lambda e: e.matmul(
                    ps[bb][:, 0:256], lhsT=woa[:, c, dc * 128:(dc + 1) * 128], rhs=aT[:, c, i * 256:(i + 1) * 256],
                    start=(c == 0), stop=(c == 3)))(bb, dc, c, i), reads=["woa", "aT%d" % i], writes=["ps%d" % bb])
            for c in range(4):
                P.add("pe", (lambda bb, dc, c: lambda e: e.matmul(
                    ps[bb][:, 256:512], lhsT=wop[:, c, dc * 128:(dc + 1) * 128], rhs=BT[:, c, :],
                    start=(c == 0), stop=(c == 3)))(bb, dc, c), reads=["wop", "BT"], writes=["ps%d" % bb])
            for ab in range(2):
                P.add("act", (lambda s, ba, ab, dc: lambda e: e.activation(
                    out=sg[s][:, ab, :], in_=ps[ba][:, ab * 256:(ab + 1) * 256], func=AF.Sigmoid,
                    bias=bgate[:, ab * 8 + dc:ab * 8 + dc + 1]))(s, ba, ab, dc),
                    reads=["ps%d" % ba, "bgate"], writes=["sg%d" % s])
            P.add("dve", (lambda s, bb: lambda e: e.tensor_tensor(
                out=t12[s], in0=sg[s], in1=ps[bb][:, :].rearrange("p (a b) -> p a b", a=2), op=ALU.mult))(s, bb),
                reads=["sg%d" % s, "ps%d" % bb], writes=["sg%d" % s])
            P.add("dve", (lambda s, dc: lambda e: e.tensor_tensor(out=uT[:, dc, :], in0=t12[s][:, 0, :], in1=t12[s][:, 1, :], op=ALU.add))(s, dc),
                  reads=["sg%d" % s], writes=["uT"])

    def Mx_mm(i):
        for tt in range(2):
            for hf in range(2):
                pb = 4 + tt * 2 + hf
                for dc in range(8):
                    P.add("pe", (lambda pb, tt, hf, dc: lambda e: e.matmul(
                        ps[pb][:, :], lhsT=uT[:, dc, tt * 128:(tt + 1) * 128], rhs=wout[:, dc, hf * 512:(hf + 1) * 512],
                        start=(dc == 0), stop=(dc == 7)))(pb, tt, hf, dc), reads=["uT", "wout"], writes=["ps%d" % pb])

    def Mx_stt(i):
        for tt in range(2):
            for hf in range(2):
                pb = 4 + tt * 2 + hf
                P.add("dve", (lambda pb, tt, hf: lambda e: e.scalar_tensor_tensor(
                    out=xf[:, tt, hf * 512:(hf + 1) * 512], in0=xf[:, tt, hf * 512:(hf + 1) * 512], scalar=ALPHA,
                    in1=ps[pb][:, :], op0=ALU.mult, op1=ALU.add))(pb, tt, hf),
                    reads=["xf%d" % tt, "ps%d" % pb], writes=["xf%d" % tt])

    def L1(i):
        layer_norm(0, 1)
        for tt in range(2):
            P.add("act", (lambda tt: lambda e: e.activation(out=hb[:, tt, :], in_=xf[:, tt, :], func=AF.Copy))(tt),
                  reads=["xf%d" % tt], writes=["xs2"])
            for kc in range(8):
                P.add("pe", (lambda tt, kc: lambda e: e.transpose(
                    out=psb[0][:, kc * 128:(kc + 1) * 128], in_=hb[:, tt, kc * 128:(kc + 1) * 128], identity=ident))(tt, kc),
                    reads=["xs2", "ident"], writes=["ps0"])
            P.add("dve", (lambda tt: lambda e: e.tensor_copy(
                out=hT[:, :, tt * 128:(tt + 1) * 128], in_=psb[0][:, :].rearrange("p (k c) -> p k c", k=8)))(tt),
                reads=["ps0"], writes=["hT%d" % tt])

    def ff2(gi, ws):
        for tt in range(2):
            for hf in range(2):
                pb = 4 + tt * 2 + hf
                for f in range(FG):
                    P.add("pe", (lambda pb, tt, hf, f, ws, gi: lambda e: e.matmul(
                        ps[pb][:, :], lhsT=rT[ws][:, f, tt * 128:(tt + 1) * 128], rhs=w2c[ws][:, f, hf * 512:(hf + 1) * 512],
                        start=(gi == 0 and f == 0), stop=(gi == NFG - 1 and f == FG - 1)))(pb, tt, hf, f, ws, gi),
                        reads=["rT%d" % ws, "w2c%d" % ws], writes=["ps%d" % pb])

    def Fw(gi):
        ws = wslot[0] % NWS
        wslot[0] += 1
        c0 = gi * FG * 128
        P.add("sp", (lambda ws, c0: lambda e: e.dma_start(out=w1c[ws], in_=wff1_b[:, :, c0:c0 + FG * 128]))(ws, c0),
              reads=["bar2b", "wsc1_%d" % (c0 // 512)], writes=["w1c%d" % ws], dma="w1c%d" % ws)
        P.add("sp", (lambda ws, gi: lambda e: e.dma_start(out=w2c[ws], in_=wff2_b[:, gi * FG:(gi + 1) * FG, :]))(ws, gi),
              reads=["bar2b", "wsc2_%d" % (gi * FG // 4)], writes=["w2c%d" % ws], dma="w2c%d" % ws)
        return ws

    def F(i):
        pend = None
        for gi in range(NFG):
            ws = Fw(gi)
            for f in range(FG):
                fc = gi * FG + f
                fb = 1 + (fbank[0] % 3)
                fbank[0] += 1
                for kc in range(8):
                    P.add("pe", (lambda fb, ws, f, kc: lambda e: e.matmul(
                        ps[fb][:, 0:256], lhsT=w1c[ws][:, kc, f * 128:(f + 1) * 128], rhs=hT[:, kc, :],
                        start=(kc == 0), stop=(kc == 7)))(fb, ws, f, kc),
                        reads=["w1c%d" % ws, "hT0", "hT1"], writes=["ps%d" % fb])
                rs = fc % 2
                P.add("act", (lambda fb, rs, fc: lambda e: e.activation(
                    out=rr[rs], in_=ps[fb][:, 0:256], func=AF.Relu, bias=bff1[:, fc:fc + 1]))(fb, rs, fc),
                    reads=["ps%d" % fb, "bff1"], writes=["rr%d" % rs])
                P.add("act", (lambda rs, ws, f: lambda e: e.activation(out=rT[ws][:, f, :], in_=rr[rs], func=AF.Square))(rs, ws, f),
                      reads=["rr%d" % rs], writes=["rT%d" % ws])
            if pend is not None:
                ff2(*pend)
            pend = (gi, ws)
        ff2(*pend)

    def L2a(i):
        for tt in range(2):
            for hf in range(2):
                pb = 4 + tt * 2 + hf
                P.add("dve", (lambda pb, tt, hf: lambda e: e.scalar_tensor_tensor(
                    out=xf[:, tt, hf * 512:(hf + 1) * 512], in0=xf[:, tt, hf * 512:(hf + 1) * 512], scalar=ALPHA,
                    in1=ps[pb][:, :], op0=ALU.mult, op1=ALU.add))(pb, tt, hf),
                    reads=["xf%d" % tt, "ps%d" % pb], writes=["xf%d" % tt])

    def L2b(i):
        for tt in range(2):
            P.add("dve", (lambda tt: lambda e: e.tensor_tensor(out=xf[:, tt, :], in0=xf[:, tt, :], in1=lnc[:, 4, :], op=ALU.add))(tt),
                  reads=["xf%d" % tt, "lnc4"], writes=["xf%d" % tt])
        layer_norm(2, 3)
        out_v = out[i * 256:(i + 1) * 256, :].rearrange("(t p) d -> p t d", p=128)
        P.add("sp", (lambda out_v: lambda e: e.dma_start(out=out_v, in_=xf))(out_v), reads=["xf0", "xf1"],
              writes=["out%d" % i], dma="outst")

    X_load(0)
    XF_load(0)
    X(0)
    Pm_a(0)
    Pm_b(0)
    G(0)
    X_load(1)
    for i in range(NBLK):
        nxt = i + 1 < NBLK
        Mx_mm(i)
        if nxt:
            X(i + 1)
            Pm_a(i + 1)
        Mx_stt(i)
        L1(i)
        if i + 2 < NBLK:
            X_load(i + 2)
        F(i)
        if nxt:
            Pm_b(i + 1)
        L2a(i)
        if nxt:
            G(i + 1)
        L2b(i)
        if nxt:
            XF_load(i + 1)
    fin = ["out%d" % i for i in range(NBLK)]
    if debug:
        dbg_aT = nc.dram_tensor("dbg_aT", [128, 4 * TOK], BF16, kind="ExternalOutput").ap()
        P.add("sp", lambda e: e.dma_start(out=dbg_aT, in_=aT.rearrange("p a b -> p (a b)")),
              reads=["aT%d" % i for i in range(NBLK)], writes=["dbg_aT"], dma="dbg_aT")
        fin.append("dbg_aT")
    P.add("sp", None, reads=fin)
    P.emit(nc)
    return nc


def _rel_bucket_np(dist):
    n = np.maximum(dist, 0)
    nf = np.maximum(n, 1).astype(np.float32)
    large = 16 + (np.log(nf / np.float32(16)) / np.float32(math.log(128 / 16)) * np.float32(16)).astype(np.int32)
    large = np.minimum(large, 31)
    return np.where(n < 16, n, large)


_CACHE = {}


def _consts():
    if "c" in _CACHE:
        return _CACHE["c"]
    ident = np.eye(128, dtype=np.float32)
    jmat = np.ascontiguousarray(ident[::-1])
    dist = np.arange(384) - 127
    bkt = _rel_bucket_np(dist.astype(np.int32))
    oht = np.zeros((33, 384), np.float32)
    for d in range(384):
        if dist[d] < 0:
            oht[32, d] = 1.0
        else:
            oht[bkt[d], d] = 1.0
    gb = np.zeros((2, 16, 32), np.float32)
    for par in range(2):
        for i in range(16):
            for c in range(32):
                valid = (c < i) if c < 16 else ((c - 16) <= i and not (par == 0 and c == 16))
                gb[par, i, c] = 0.0 if valid else NEG
    rc = np.zeros((2, 4, 16), np.float32)
    for par in range(2):
        for g, w in enumerate(WINS):
            for t in range(16):
                rc[par, g, t] = 1.0 / (min(t + 1, w) if par == 0 else w)
    c = dict(ident=ident, jmat=jmat, oht=oht,
             gbias=[np.ascontiguousarray(np.broadcast_to(gb[p].reshape(1, 512), (128, 512))) for p in range(2)],
             rc0=[np.ascontiguousarray(np.broadcast_to(rc[p].reshape(1, 64), (128, 64))) for p in range(2)])
    _CACHE["c"] = c
    return c


def kernel(x, w_in, b_gate, rel_table, w_pool, pool_scale, w_o_attn, w_o_pool, w_out,
           ln1_g, ln1_b, w_ff1, b_ff1, w_ff2, b_ff2, ln2_g, ln2_b):
    f = lambda a: np.ascontiguousarray(np.asarray(a, dtype=np.float32))
    x = f(x)
    c = _consts()
    if "nc" not in _CACHE:
        _CACHE["nc"] = build_program()
    nc = _CACHE["nc"]
    shared = dict(
        w_in=f(w_in[0]), w_o_attn=f(w_o_attn[0]), w_o_pool=f(w_o_pool[0]), w_out=f(w_out[0]),
        w_ff1=f(w_ff1[0]), w_ff2=f(w_ff2[0]), w_pool=f(w_pool[0]), rel_table=f(rel_table),
        lnvec=f(np.stack([np.asarray(ln1_g)[0], np.asarray(ln1_b)[0], np.asarray(ln2_g)[0], np.asarray(ln2_b)[0],
                          np.asarray(b_ff2)[0]])),
        bgate_l=f(np.asarray(b_gate)[0].reshape(16, 128).T), bff1_l=f(np.asarray(b_ff1)[0].reshape(32, 128).T),
        pscale_l=f(np.asarray(pool_scale)[0].reshape(4, 128).T),
        ident_f=c["ident"], jmat_f=c["jmat"], oht_f=c["oht"])
    in_maps = []
    for core in range(8):
        b, par = core // 2, core % 2
        xb = x[b].reshape(32, 256, D)
        own = xb[par::2]
        if par == 0:
            oth = np.concatenate([np.zeros((1, 256, D), np.float32), xb[1::2][:15]], axis=0)
        else:
            oth = xb[0::2]
        m = dict(shared)
        m["x_own"] = np.ascontiguousarray(own.reshape(TOK, D))
        m["x_oth"] = np.ascontiguousarray(oth.reshape(TOK, D))
        m["gbias_f"] = c["gbias"][par]
        m["rc0_f"] = c["rc0"][par]
        in_maps.append(m)
    res = run_bass_kernel_spmd(nc, in_maps, core_ids=list(range(8)))
    outp = np.empty((4, 32, 256, D), np.float32)
    for core in range(8):
        b, par = core // 2, core % 2
        outp[b, par::2] = np.asarray(res.results[core]["out"], dtype=np.float32).reshape(16, 256, D)
    return outp.reshape(4, 8192, D)
```
